# Optimizing a Trainium2 kernel written in Bass

```python
import math
import jax, jax.numpy as jnp
from jax import lax
import numpy as np

D_MODEL = 1024
BATCH = 2
SEQ = 8192
DEPTH = 1

PLE_DIM = 256
DIFF_HEADS = 4
DIFF_HEAD_DIM = 64
DIFF_WIDTH = DIFF_HEADS * 2 * DIFF_HEAD_DIM
MOBA_HEADS = 8
MOBA_HEAD_DIM = 64
MOBA_WIDTH = MOBA_HEADS * MOBA_HEAD_DIM
MOBA_BLOCK = 256
MOBA_TOPK = 3
ROT_DIM = 16
ROPE_THETA = 500000.0
Q_CHUNK = 128
N_BRANCHES = 2
IN_COLS = 4 * DIFF_WIDTH + 4 * MOBA_WIDTH + N_BRANCHES * D_MODEL
EPS = 1e-6
SUBLN_EPS = 1e-5

kernel_name = "hybrid_diffattn_moba_gated_block"


def rms_norm(x, g, eps=EPS):
    xf = x.astype(jnp.float32)
    y = xf * lax.rsqrt(jnp.mean(xf * xf, axis=-1, keepdims=True) + eps)
    return (y * g.astype(jnp.float32)).astype(x.dtype)


def rope_tables(seq):
    inv = ROPE_THETA ** (-jnp.arange(0, ROT_DIM, 2, dtype=jnp.float32) / ROT_DIM)
    ang = jnp.arange(seq, dtype=jnp.float32)[:, None] * inv[None, :]
    return jnp.cos(ang), jnp.sin(ang)


def partial_rope(x, cos, sin):
    half = ROT_DIM // 2
    shape = (1, cos.shape[0]) + (1,) * (x.ndim - 3) + (half,)
    c = cos.reshape(shape).astype(x.dtype)
    s = sin.reshape(shape).astype(x.dtype)
    x1 = x[..., :half]
    x2 = x[..., half:ROT_DIM]
    rot = jnp.concatenate([x1 * c - x2 * s, x2 * c + x1 * s], axis=-1)
    return jnp.concatenate([rot, x[..., ROT_DIM:]], axis=-1)


def split_in_proj(proj):
    sizes = (DIFF_WIDTH,) * 4 + (MOBA_WIDTH,) * 4 + (N_BRANCHES * D_MODEL,)
    idx = []
    acc = 0
    for s in sizes[:-1]:
        acc += s
        idx.append(acc)
    return jnp.split(proj, idx, axis=-1)


def diff_attention(q, k, v, lam, lam_init, subln_g):
    B, S, H, _, dh = q.shape
    n_chunks = S // Q_CHUNK
    scale = dh ** -0.5
    key_pos = jnp.arange(S)
    qc = q.reshape(B, n_chunks, Q_CHUNK, H, 2, dh).transpose(1, 0, 2, 3, 4, 5)

    def one_chunk(args):
        c, q_blk = args
        s = jnp.einsum('bqhmd,bkhmd->bhmqk', q_blk, k).astype(jnp.float32) * scale
        q_pos = c * Q_CHUNK + jnp.arange(Q_CHUNK)
        s = jnp.where(key_pos[None, :] <= q_pos[:, None], s, -jnp.inf)
        a = jax.nn.softmax(s, axis=-1)
        w = a[:, :, 0] - lam * a[:, :, 1]
        return jnp.einsum('bhqk,bkhe->bqhe', w.astype(v.dtype), v)

    o = lax.map(one_chunk, (jnp.arange(n_chunks), qc))
    o = o.transpose(1, 0, 2, 3, 4).reshape(B, S, H, 2 * dh)
    o = rms_norm(o, subln_g, SUBLN_EPS) * (1.0 - lam_init)
    return o.reshape(B, S, H * 2 * dh)


def moba_attention(q, k, v):
    B, S, H, dh = q.shape
    nb = -(-S // MOBA_BLOCK)
    pad = nb * MOBA_BLOCK - S
    kp = jnp.pad(k, ((0, 0), (0, pad), (0, 0), (0, 0)))
    vp = jnp.pad(v, ((0, 0), (0, pad), (0, 0), (0, 0)))
    k_blocks = kp.reshape(B, nb, MOBA_BLOCK, H, dh).transpose(0, 3, 1, 2, 4)
    v_blocks = vp.reshape(B, nb, MOBA_BLOCK, H, dh).transpose(0, 3, 1, 2, 4)
    k_mean = jnp.mean(k_blocks.astype(jnp.float32), axis=3).astype(k.dtype)
    topk = min(MOBA_TOPK, nb)
    scale = dh ** -0.5
    n_chunks = S // Q_CHUNK
    qc = q.reshape(B, n_chunks, Q_CHUNK, H, dh).transpose(1, 0, 3, 2, 4)
    b_idx = jnp.arange(B)[:, None, None, None]
    h_idx = jnp.arange(H)[None, :, None, None]
    blk_ids = jnp.arange(nb)

    def one_chunk(args):
        c, q_blk = args
        q_start = c * Q_CHUNK
        own = q_start // MOBA_BLOCK
        q_pos = q_start + jnp.arange(Q_CHUNK)
        gate = jnp.einsum('bhqd,bhnd->bhqn', q_blk, k_mean).astype(jnp.float32)
        gate = jnp.where(blk_ids < own, gate, -jnp.inf)
        _, sel = lax.top_k(gate, topk)
        sel_valid = jnp.arange(topk) < own
        k_sel = k_blocks[b_idx, h_idx, sel]
        v_sel = v_blocks[b_idx, h_idx, sel]
        s_sel = jnp.einsum('bhqd,bhqrkd->bhqrk', q_blk, k_sel).astype(jnp.float32) * scale
        s_sel = jnp.where(sel_valid[:, None], s_sel, -jnp.inf).reshape(B, H, Q_CHUNK, topk * MOBA_BLOCK)
        k_own = lax.dynamic_index_in_dim(k_blocks, own, axis=2, keepdims=False)
        v_own = lax.dynamic_index_in_dim(v_blocks, own, axis=2, keepdims=False)
        s_own = jnp.einsum('bhqd,bhkd->bhqk', q_blk, k_own).astype(jnp.float32) * scale
        own_pos = own * MOBA_BLOCK + jnp.arange(MOBA_BLOCK)
        s_own = jnp.where(own_pos[None, :] <= q_pos[:, None], s_own, -jnp.inf)
        pr = jax.nn.softmax(jnp.concatenate([s_sel, s_own], axis=-1), axis=-1).astype(v.dtype)
        p_sel = pr[..., :topk * MOBA_BLOCK].reshape(B, H, Q_CHUNK, topk, MOBA_BLOCK)
        p_own = pr[..., topk * MOBA_BLOCK:]
        return (jnp.einsum('bhqrk,bhqrkd->bhqd', p_sel, v_sel)
                + jnp.einsum('bhqk,bhkd->bhqd', p_own, v_own))

    o = lax.map(one_chunk, (jnp.arange(n_chunks), qc))
    return o.transpose(1, 0, 3, 2, 4).reshape(B, S, H * dh)


def setup_inputs(seed: int = 0) -> dict:
    key = jax.random.key(seed)
    ks = jax.random.split(key, 16)
    f32 = jnp.float32
    nrm = lambda k, shape, sc: jax.random.normal(k, shape, f32) * sc
    return {
        'x': nrm(ks[0], (BATCH, SEQ, D_MODEL), 1.0),
        'p': nrm(ks[1], (DEPTH, BATCH, SEQ, PLE_DIM), 1.0),
        'norm_g': 1.0 + nrm(ks[2], (DEPTH, D_MODEL), 0.02),
        'w_in': nrm(ks[3], (DEPTH, D_MODEL, IN_COLS), D_MODEL ** -0.5),
        'lambda_q1': nrm(ks[4], (DEPTH, DIFF_HEAD_DIM), 0.1),
        'lambda_k1': nrm(ks[5], (DEPTH, DIFF_HEAD_DIM), 0.1),
        'lambda_q2': nrm(ks[6], (DEPTH, DIFF_HEAD_DIM), 0.1),
        'lambda_k2': nrm(ks[7], (DEPTH, DIFF_HEAD_DIM), 0.1),
        'subln_g': 1.0 + nrm(ks[8], (DEPTH, 2 * DIFF_HEAD_DIM), 0.02),
        'w_branch_diff': nrm(ks[9], (DEPTH, DIFF_WIDTH, D_MODEL), DIFF_WIDTH ** -0.5),
        'w_branch_moba': nrm(ks[10], (DEPTH, MOBA_WIDTH, D_MODEL), MOBA_WIDTH ** -0.5),
        'w_out': nrm(ks[11], (DEPTH, D_MODEL, D_MODEL), D_MODEL ** -0.5),
        'w_ple': nrm(ks[12], (DEPTH, PLE_DIM, D_MODEL), PLE_DIM ** -0.5),
        'w_ple_gate': nrm(ks[13], (DEPTH, D_MODEL, D_MODEL), D_MODEL ** -0.5),
        'final_g': 1.0 + nrm(ks[14], (D_MODEL,), 0.02),
    }


def reference(x, p, norm_g, w_in, lambda_q1, lambda_k1, lambda_q2, lambda_k2, subln_g,
              w_branch_diff, w_branch_moba, w_out, w_ple, w_ple_gate, final_g):
    B, S, _ = x.shape
    cos, sin = rope_tables(S)
    for i in range(DEPTH):
        lam_init = 0.8 - 0.6 * math.exp(-0.3 * i)
        h = rms_norm(x, norm_g[i])
        proj = h @ w_in[i]
        dq, dk, dv, dg, mq, mk, mv, mg, gates = split_in_proj(proj)
        dq = partial_rope(dq.reshape(B, S, DIFF_HEADS, 2, DIFF_HEAD_DIM), cos, sin)
        dk = partial_rope(dk.reshape(B, S, DIFF_HEADS, 2, DIFF_HEAD_DIM), cos, sin)
        dv = dv.reshape(B, S, DIFF_HEADS, 2 * DIFF_HEAD_DIM)
        lam = (jnp.exp(jnp.sum(lambda_q1[i].astype(jnp.float32) * lambda_k1[i].astype(jnp.float32)))
               - jnp.exp(jnp.sum(lambda_q2[i].astype(jnp.float32) * lambda_k2[i].astype(jnp.float32)))
               + lam_init)
        o_a = diff_attention(dq, dk, dv, lam, lam_init, subln_g[i]) * jax.nn.silu(dg)
        y_a = o_a @ w_branch_diff[i]
        mq = partial_rope(mq.reshape(B, S, MOBA_HEADS, MOBA_HEAD_DIM), cos, sin)
        mk = partial_rope(mk.reshape(B, S, MOBA_HEADS, MOBA_HEAD_DIM), cos, sin)
        mv = mv.reshape(B, S, MOBA_HEADS, MOBA_HEAD_DIM)
        o_b = moba_attention(mq, mk, mv) * jax.nn.silu(mg)
        y_b = o_b @ w_branch_moba[i]
        g_a, g_b = jnp.split(gates, N_BRANCHES, axis=-1)
        merged = jax.nn.sigmoid(g_a) * y_a + jax.nn.sigmoid(g_b) * y_b
        x = x + merged @ w_out[i]
        x = x + jax.nn.sigmoid(x @ w_ple_gate[i]) * (p[i] @ w_ple[i])
    return rms_norm(x, final_g)
```

```python
import math
from contextlib import ExitStack

import ml_dtypes
import numpy as np

import concourse.bass as bass
import concourse.mybir as mybir
from concourse.bass_utils import run_bass_kernel_spmd

F32 = mybir.dt.float32
BF16 = mybir.dt.bfloat16
I32 = mybir.dt.int32
AF = mybir.ActivationFunctionType
ALU = mybir.AluOpType
AX = mybir.AxisListType
NPBF = ml_dtypes.bfloat16

D = 1024
S = 8192
B = 2
PLE = 256
TT = 512
NT = S // TT
BIG = 30000.0
EPS = 1e-6
SUBLN_EPS = 1e-5
LAM_INIT = 0.8 - 0.6 * math.exp(-0.3 * 0)
N_CORES = 8


class Res:
    __slots__ = ("name", "w", "r", "dsem", "dcnt")

    def __init__(self, name):
        self.name = name
        self.w = None
        self.r = {}
        self.dsem = None
        self.dcnt = 0


class EngQ:
    def __init__(self, P, eng, name):
        self.eng = eng
        self.name = name
        self.sem = P.new_sem("s_" + name)
        self.cnt = 0
        self.seen = {}
        self.pending = False


class Prog:
    def __init__(self, nc):
        self.nc = nc
        self.es = ExitStack()
        self.scopes = [self.es]
        self.pe = EngQ(self, nc.tensor, "pe")
        self.act = EngQ(self, nc.scalar, "act")
        self.dve = EngQ(self, nc.vector, "dve")
        self.pool = EngQ(self, nc.gpsimd, "pool")
        self.sp = EngQ(self, nc.sync, "sp")
        self.engs = [self.pe, self.act, self.dve, self.pool, self.sp]

    def new_sem(self, name):
        return self.es.enter_context(self.nc.semaphore(name))

    def push_scope(self):
        self.scopes.append(ExitStack())

    def pop_scope(self):
        self.scopes.pop().close()

    def sbuf(self, name, shape, dt):
        return self.scopes[-1].enter_context(self.nc.sbuf_tensor(name, shape, dt))

    def psum(self, name, shape, dt):
        return self.scopes[-1].enter_context(self.nc.psum_tensor(name, shape, dt))

    def barrier(self, dma_res=()):
        for X in self.engs:
            assert not X.pending, X.name
        for E in self.engs:
            for X in self.engs:
                if X is not E and X.cnt > 0 and E.seen.get(X.name, 0) < X.cnt:
                    E.eng.wait_ge(X.sem, X.cnt)
                    E.seen[X.name] = X.cnt
            for r in dma_res:
                key = "d_" + r.name
                if r.dcnt > 0 and E.seen.get(key, 0) < r.dcnt:
                    E.eng.wait_ge(r.dsem, r.dcnt)
                    E.seen[key] = r.dcnt

    def res(self, name, dma=False):
        r = Res(name)
        if dma:
            r.dsem = self.new_sem("d_" + name)
        return r

    def _waits(self, E, reads, writes):
        need = {}

        def add(ev):
            if ev is None:
                return
            sem, val, key = ev
            if key not in need or need[key][1] < val:
                need[key] = (sem, val)

        for b in reads:
            add(b.w)
        for b in writes:
            add(b.w)
            for ev in b.r.values():
                add(ev)
        for key, (sem, val) in need.items():
            if key == E.name and val > E.cnt:
                continue
            if E.seen.get(key, 0) < val:
                E.eng.wait_ge(sem, val)
                E.seen[key] = val

    def _record(self, ev, reads, writes):
        key = ev[2]
        for b in reads:
            old = b.r.get(key)
            if old is None or old[1] < ev[1]:
                b.r[key] = ev
        for b in writes:
            b.w = ev
            b.r = {}

    def op(self, E, fn, reads=(), writes=(), signal=True):
        self._waits(E, reads, writes)
        ins = fn(E.eng)
        if signal:
            E.cnt += 1
            ins.then_inc(E.sem, 1)
            ev = (E.sem, E.cnt, E.name)
            E.pending = False
        else:
            ev = (E.sem, E.cnt + 1, E.name)
            E.pending = True
        self._record(ev, reads, writes)
        return ev

    def dma(self, E, out, in_, dres, reads=(), writes=(), **kw):
        assert not E.pending
        self._waits(E, reads, writes)
        ins = E.eng.dma_start(out=out, in_=in_, **kw)
        dres.dcnt += 16
        ins.then_inc(dres.dsem, 16)
        ev = (dres.dsem, dres.dcnt, "d_" + dres.name)
        self._record(ev, reads, writes)
        return ev

    def finish(self, final_res):
        E = self.sp
        self._waits(E, [], final_res)
        for X in self.engs:
            assert not X.pending, X.name
            if X is not E and X.cnt > 0 and E.seen.get(X.name, 0) < X.cnt:
                E.eng.wait_ge(X.sem, X.cnt)
                E.seen[X.name] = X.cnt

    def close(self):
        self.es.close()


CF_NG = 0
CF_SG = 8
CF_LQ1 = 9
CF_LK1 = 73
CF_LQ2 = 137
CF_LK2 = 201
CF_PM = 265
CF_N = 393
CB_ID = 0
CB_CM = 128
CB_PAST = 2176
CB_BP = 3200
CB_N = 4224


STOP = None
VSTEPS = 4


class _Stop(Exception):
    pass


def _stage(k):
    if STOP is not None and STOP == k:
        raise _Stop()


def build_phase1(nc, P, dr):
    pe, act, dve, pool, sp = P.pe, P.act, P.dve, P.pool, P.sp
    sb, ps, R = P.sbuf, P.psum, P.res

    KD = sb("KD", [128, S], BF16)
    KM2 = sb("KM2", [128, 2, S], BF16); rKM2 = R("KM2", dma=True)
    rKDt = [R(f"KD{i}") for i in range(S // TT)]
    rKMt = [R(f"KM{i}") for i in range(S // TT)]
    rVVt = [R(f"VV{i}") for i in range(S // TT)]
    VD = sb("VD", [128, S // 128, 128], BF16)
    VA = sb("VA", [128, S // 128, 128], BF16)
    VB = sb("VB", [128, S // 128, 128], BF16)
    ACCs = [sb(f"ACC{i}", [128, TT], F32) for i in range(3)]
    rACCs = [R(f"ACC{i}") for i in range(3)]
    ONESF = sb("ONESF1", [128, 128], F32); rONESF = R("ONESF1")
    X = sb("X", [128, 8, TT], F32); rX = R("X", dma=True)
    SQs = [sb(f"SQ{i}", [128, TT], BF16) for i in range(2)]
    rSQs = [R(f"SQ{i}") for i in range(2)]
    XN = sb("XN", [128, 8, TT], BF16); rXN = R("XN")
    W = sb("W", [128, 8, 1024], BF16)
    rWs = [R(f"W{i}", dma=True) for i in range(4)]
    CF = sb("CF", [128, CF_N], F32); rCF = R("CF", dma=True)
    CB = sb("CB", [128, CB_N], BF16); rCB = R("CB", dma=True)
    CS = sb("CS", [128, 2, TT], F32); rCS = R("CS", dma=True)
    QDs = [sb(f"QD{i}", [128, 2, TT], BF16) for i in range(2)]
    rQDs = [R(f"QD{i}") for i in range(2)]
    QMs = [sb(f"QM{i}", [128, 2, TT], BF16) for i in range(2)]
    rQMs = [R(f"QM{i}") for i in range(2)]
    KMN = sb("KMN", [128, 32], BF16); rKMN = R("KMN")
    KMF = sb("KMF", [128, 2], F32); rKMF = R("KMF")
    ONES = sb("ONES", [128, 128], BF16); rONES = R("ONES")
    ONESN = sb("ONESN", [128, 128], BF16); rONESN = R("ONESN")
    ONESH = sb("ONESH", [128, 128], BF16); rONESH = R("ONESH")
    EPSV = sb("EPSV", [128, 2], F32); rEPSV = R("EPSV")
    ONEV = sb("ONEV", [128, 1], F32); rONEV = R("ONEV")
    LAMV = sb("LAMV", [128, 8], F32); rLAMV = R("LAMV")
    LTMP = sb("LTMP", [128, 64], F32); rLTMP = R("LTMP")
    NPT = 6
    PTs = [sb(f"PT{i}", [128, TT], BF16) for i in range(NPT)]
    rPTs = [R(f"PT{i}") for i in range(NPT)]
    QSs = [sb(f"QS{i}", [128, TT], F32) for i in range(2)]
    rQSs = [R(f"QS{i}") for i in range(2)]
    T1 = sb("T1", [128, TT], F32); rT1 = R("T1")
    T2 = sb("T2", [128, TT], F32); rT2 = R("T2")
    SD = sb("SD", [128, TT], F32); rSD = R("SD")
    RS = sb("RS", [128, TT], F32); rRS = R("RS")
    GDs = [sb(f"GD{i}", [128, TT], F32) for i in range(2)]
    rGDs = [R(f"GD{i}") for i in range(2)]
    GMs = [sb(f"GM{i}", [128, TT], F32) for i in range(2)]
    rGMs = [R(f"GM{i}") for i in range(2)]
    A0 = sb("A0", [128, TT], F32); rA0 = R("A0")
    A1 = sb("A1", [128, TT], F32); rA1 = R("A1")
    RC = sb("RC", [128, TT], F32); rRC = R("RC")
    OO = sb("OO", [128, TT], F32); rOO = R("OO")
    SQ2 = sb("SQ2", [128, TT], BF16); rSQ2 = R("SQ2")
    OUTA = sb("OUTA", [128, TT], BF16); rOUTA = R("OUTA", dma=True)
    OUTB = sb("OUTB", [128, TT], BF16); rOUTB = R("OUTB", dma=True)
    M8 = sb("M8", [128, 8, 8], F32); rM8 = R("M8")
    PENF = sb("PENF", [128, 8, 32], F32); rPENF = R("PENF")
    PENB = sb("PENB", [128, 8, 32], BF16); rPENB = R("PENB")

    NSB = 3
    SP_ = [ps(f"S{i}", [128, TT], F32) for i in range(NSB)]
    rSP = [R(f"S{i}") for i in range(NSB)]
    OP = ps("OP", [128, TT], F32); rOP = R("OP")
    LP = ps("LP", [128, TT], F32); rLP = R("LP")
    PJ = [ps("PJ0", [128, TT], F32), ps("PJ1", [128, TT], F32)]
    rPJ = [R("PJ0"), R("PJ1")]
    MS = ps("MS", [128, TT], F32); rMS = R("MS")
    GT = MS[:, 0:256]; rGT = rMS
    PTR = MS[:, 256:512].bitcast(BF16); rPTR = rMS
    assert tuple(PTR.shape) == (128, TT), PTR.shape

    xT, w1, cf, cb, cs, oh = (dr[k] for k in ("xT", "w1", "cf", "cb", "cs", "oh"))
    out_fn = dr["out_fn"]

    P.dma(sp, CF[:], cf, rCF, writes=[rCF])
    P.dma(sp, CB[:], cb, rCB, writes=[rCB])
    P.dma(sp, X[:], xT[:, :, 0:TT], rX, writes=[rX])
    P.dma(sp, CS[:], cs[:, :, 0:TT], rCS, writes=[rCS])
    for c in range(4):
        P.dma(pool, W[:, :, c * 256:(c + 1) * 256], w1[:, :, c * 256:(c + 1) * 256], rWs[c], writes=[rWs[c]])
    P.op(pool, lambda e: e.memset(ONES[:], 1.0), writes=[rONES])
    P.op(pool, lambda e: e.memset(ONESF[:], 1.0), writes=[rONESF])
    P.op(pool, lambda e: e.memset(ONESN[:], 1.0 / 1024.0), writes=[rONESN])
    P.op(pool, lambda e: e.memset(ONESH[:], 1.0 / 128.0), writes=[rONESH])
    P.op(pool, lambda e: e.memset(EPSV[:, 0:1], EPS), writes=[rEPSV])
    P.op(pool, lambda e: e.memset(ONEV[:], 1.0), writes=[rONEV])
    P.op(pool, lambda e: e.memset(EPSV[:, 1:2], SUBLN_EPS), writes=[rEPSV])
    P.op(pool, lambda e: e.memset(KM2[32:64, 1, :], 0.0), writes=[rKM2] + rKMt)
    P.op(pool, lambda e: e.memset(KMN[:], 0.0), writes=[rKMN])
    for i in range(2):
        P.op(pool, lambda e, i=i: e.memset(QMs[i][:], 0.0), writes=[rQMs[i]])
        P.op(pool, lambda e, i=i: e.memset(QDs[i][:], 0.0), writes=[rQDs[i]])
    P.dma(sp, KM2[64:96, 0, :], oh, rKM2, writes=[rKM2] + rKMt)
    P.dma(sp, KM2[0:32, 1, :], oh, rKM2, writes=[rKM2] + rKMt)
    for i, (a, b_) in enumerate(((CF_LQ1, CF_LK1), (CF_LQ2, CF_LK2))):
        P.op(dve, lambda e, a=a, b_=b_: e.tensor_tensor(out=LTMP[:], in0=CF[:, a:a + 64], in1=CF[:, b_:b_ + 64], op=ALU.mult),
             reads=[rCF], writes=[rLTMP])
        P.op(dve, lambda e, i=i: e.tensor_reduce(out=LAMV[:, 2 + i:3 + i], in_=LTMP[:], axis=AX.X, op=ALU.add),
             reads=[rLTMP], writes=[rLAMV])
        P.op(act, lambda e, i=i: e.activation(out=LAMV[:, 4 + i:5 + i], in_=LAMV[:, 2 + i:3 + i], func=AF.Exp),
             reads=[rLAMV], writes=[rLAMV])
    P.op(dve, lambda e: e.tensor_tensor(out=LAMV[:, 6:7], in0=LAMV[:, 5:6], in1=LAMV[:, 4:5], op=ALU.subtract),
         reads=[rLAMV], writes=[rLAMV])
    P.op(dve, lambda e: e.tensor_scalar(out=LAMV[:, 0:1], in0=LAMV[:, 6:7], scalar1=-LAM_INIT, scalar2=None, op0=ALU.add),
         reads=[rLAMV], writes=[rLAMV])
    P.op(dve, lambda e: e.tensor_scalar(out=LAMV[:, 1:2], in0=CF[:, CF_SG:CF_SG + 1], scalar1=1.0 - LAM_INIT, scalar2=None, op0=ALU.mult),
         reads=[rCF], writes=[rLAMV])

    IDENT = CB[:, CB_ID:CB_ID + 128]
    pj_i = [0]
    _stage(0)

    def next_pj():
        i = pj_i[0]
        pj_i[0] ^= 1
        return PJ[i], rPJ[i]

    def proj_group(col):
        pj, rpj = next_pj()
        for kc in range(8):
            P.op(pe, lambda e, kc=kc: e.matmul(pj[:], lhsT=W[:, kc, col * 128:(col + 1) * 128], rhs=XN[:, kc, :],
                                              start=(kc == 0), stop=(kc == 7)),
                 reads=[rWs[col // 2], rXN], writes=[rpj], signal=(kc == 7))
        return pj, rpj

    def rope_head(col, qi):
        pj, rpj = proj_group(col)
        P.op(act, lambda e: e.activation(out=QSs[qi][:], in_=pj[:], func=AF.Copy), reads=[rpj], writes=[rQSs[qi]])

    def rope_tail(qi, outs):
        QS_, rQS_ = QSs[qi], rQSs[qi]
        P.op(pe, lambda e: e.matmul(MS[:], lhsT=CF[:, CF_PM:CF_PM + 128], rhs=QS_[:], start=True, stop=True),
             reads=[rCF, rQS_], writes=[rMS])
        P.op(dve, lambda e: e.tensor_tensor(out=T1[:], in0=QS_[:], in1=CS[:, 0, :], op=ALU.mult),
             reads=[rQS_, rCS], writes=[rT1])
        P.op(dve, lambda e: e.tensor_tensor(out=T2[:], in0=MS[:], in1=CS[:, 1, :], op=ALU.mult),
             reads=[rMS, rCS], writes=[rT2])
        for (oap, psl, ores) in outs:
            P.op(dve, lambda e, oap=oap, psl=psl: e.tensor_tensor(out=oap, in0=T1[psl, :], in1=T2[psl, :], op=ALU.add),
                 reads=[rT1, rT2], writes=[ores])

    def frontend(t):
        c0 = t * TT
        for kc in range(8):
            i = kc % 2
            P.op(act, lambda e, kc=kc, i=i: e.activation(out=SQs[i][:], in_=X[:, kc, :], func=AF.Square), reads=[rX], writes=[rSQs[i]])
            P.op(pe, lambda e, kc=kc, i=i: e.matmul(MS[:], lhsT=ONESN[:], rhs=SQs[i][:], start=(kc == 0), stop=(kc == 7)),
                 reads=[rONESN, rSQs[i]], writes=[rMS])
        P.op(act, lambda e: e.activation(out=SD[:], in_=MS[:], func=AF.Ln, bias=EPSV[:, 0:1], scale=1.0),
             reads=[rMS, rEPSV], writes=[rSD])
        P.op(act, lambda e: e.activation(out=RS[:], in_=SD[:], func=AF.Exp, scale=-0.5), reads=[rSD], writes=[rRS])
        for kc in range(8):
            P.op(dve, lambda e, kc=kc: e.scalar_tensor_tensor(out=XN[:, kc, :], in0=X[:, kc, :], scalar=CF[:, CF_NG + kc:CF_NG + kc + 1],
                                                            in1=RS[:], op0=ALU.mult, op1=ALU.mult),
                 reads=[rX, rCF, rRS], writes=[rXN])
        if t + 1 < NT:
            P.dma(sp, X[:], xT[:, :, c0 + TT:c0 + 2 * TT], rX, writes=[rX])

    def projA(t):
        c0 = t * TT
        cols = slice(c0, c0 + TT)
        QD, rQD = QDs[t % 2], rQDs[t % 2]
        QM, rQM = QMs[t % 2], rQMs[t % 2]
        outs = [
            [(QD[0:64, 0, :], slice(0, 64), rQD), (QD[64:128, 1, :], slice(64, 128), rQD)],
            [(KD[:, cols], slice(0, 128), rKDt[t])],
            [(QM[0:64, 0, :], slice(0, 64), rQM), (QM[64:128, 1, :], slice(64, 128), rQM)],
            [(KM2[0:64, 0, cols], slice(0, 64), rKMt[t]), (KM2[64:128, 1, cols], slice(64, 128), rKMt[t])],
        ]
        rope_head(0, 0)
        rope_head(1, 1)
        rope_tail(0, outs[0])
        rope_head(2, 0)
        rope_tail(1, outs[1])
        rope_head(3, 1)
        rope_tail(0, outs[2])
        rope_tail(1, outs[3])
        if t + 1 < NT:
            P.dma(sp, CS[:], cs[:, :, c0 + TT:c0 + 2 * TT], rCS, writes=[rCS])

    def projB(t):
        cols = slice(t * TT, (t + 1) * TT)
        GD, rGD = GDs[t % 2], rGDs[t % 2]
        GM, rGM = GMs[t % 2], rGMs[t % 2]
        pj, rpj = proj_group(4)
        P.op(act, lambda e: e.activation(out=GD[:], in_=pj[:], func=AF.Silu), reads=[rpj], writes=[rGD])
        pj, rpj = proj_group(5)
        P.op(act, lambda e: e.activation(out=GM[:], in_=pj[:], func=AF.Silu), reads=[rpj], writes=[rGM])
        for st in range(4):
            pj, rpj = next_pj()
            kt = t * 4 + st
            for kc in range(8):
                P.op(pe, lambda e, kc=kc, st=st: e.matmul(pj[:, 0:192], lhsT=XN[:, kc, st * 128:(st + 1) * 128], rhs=W[:, kc, 768:960],
                                                        start=(kc == 0), stop=(kc == 7)),
                     reads=[rWs[3], rXN], writes=[rpj], signal=False)
            for kc in range(8):
                P.op(pe, lambda e, kc=kc, st=st: e.matmul(pj[:, 320:384], lhsT=XN[:, kc, st * 128:(st + 1) * 128], rhs=W[:, kc, 960:1024],
                                                        start=(kc == 0), stop=(kc == 7)),
                     reads=[rWs[3], rXN], writes=[rpj], signal=False)
            P.op(pe, lambda e: e.matmul(pj[:, 192:320], lhsT=ONES[0:1, :], rhs=ONES[0:1, :], start=True, stop=True),
                 reads=[rONES], writes=[rpj])
            P.op(dve, lambda e, kt=kt: e.tensor_copy(out=VD[:, kt, :], in_=pj[:, 0:128]), reads=[rpj], writes=[rVVt[t]])
            P.op(dve, lambda e, kt=kt: e.tensor_copy(out=VA[:, kt, :], in_=pj[:, 128:256]), reads=[rpj], writes=[rVVt[t]])
            P.op(dve, lambda e, kt=kt: e.tensor_copy(out=VB[:, kt, :], in_=pj[:, 256:384]), reads=[rpj], writes=[rVVt[t]])
        P.op(dve, lambda e: e.tensor_reduce(out=KMF[0:64, :], in_=KM2[0:64, 0, cols].rearrange("p (b k) -> p b k", k=256),
                                            axis=AX.X, op=ALU.add), reads=[rKMt[t]], writes=[rKMF])
        P.op(dve, lambda e: e.tensor_reduce(out=KMF[64:128, :], in_=KM2[64:128, 1, cols].rearrange("p (b k) -> p b k", k=256),
                                            axis=AX.X, op=ALU.add), reads=[rKMt[t]], writes=[rKMF])
        P.op(dve, lambda e: e.tensor_scalar(out=KMN[:, 2 * t:2 * t + 2], in0=KMF[:], scalar1=1.0 / 256.0, scalar2=None, op0=ALU.mult),
             reads=[rKMF], writes=[rKMN])

    def gatingA(t):
        QM, rQM = QMs[t % 2], rQMs[t % 2]
        for ci in range(4):
            own = (4 * t + ci) // 2
            for h in range(2):
                rows = slice(0, 64) if h == 0 else slice(64, 128)
                j = ci * 2 + h
                P.op(pe, lambda e, rows=rows, h=h, ci=ci, j=j: e.matmul(GT[:, j * 32:(j + 1) * 32], lhsT=QM[rows, h, ci * 128:(ci + 1) * 128],
                                                                      rhs=KMN[rows, :], start=True, stop=False),
                     reads=[rQM, rKMN], writes=[rGT], signal=False)
                P.op(pe, lambda e, own=own, j=j: e.matmul(GT[:, j * 32:(j + 1) * 32], lhsT=ONES[0:1, :],
                                                        rhs=CB[0:1, CB_BP + own * 32:CB_BP + (own + 1) * 32], start=False, stop=True),
                     reads=[rONES, rCB], writes=[rGT])
        for j in range(8):
            P.op(dve, lambda e, j=j: e.max(out=M8[:, j, :], in_=GT[:, j * 32:(j + 1) * 32]), reads=[rGT], writes=[rM8])
        for j in range(8):
            P.op(dve, lambda e, j=j: e.tensor_scalar(out=PENF[:, j, :], in0=GT[:, j * 32:(j + 1) * 32], scalar1=M8[:, j, 2:3], scalar2=-BIG,
                                                    op0=ALU.is_lt, op1=ALU.mult), reads=[rGT, rM8], writes=[rPENF])
        for j in range(8):
            own = (4 * t + j // 2) // 2
            P.op(dve, lambda e, j=j, own=own: e.tensor_tensor(out=PENB[:, j, :], in0=PENF[:, j, :],
                                                            in1=CB[:, CB_PAST + own * 32:CB_PAST + (own + 1) * 32], op=ALU.mult),
                 reads=[rPENF, rCB], writes=[rPENB])

    def gatingB(t):
        QM, rQM = QMs[t % 2], rQMs[t % 2]
        for ci in range(4):
            P.op(pe, lambda e, ci=ci: e.transpose(PTR[64:96, ci * 128:(ci + 1) * 128], in_=PENB[:, ci * 2, :], identity=IDENT),
                 reads=[rPENB, rCB], writes=[rPTR])
            P.op(pe, lambda e, ci=ci: e.transpose(PTR[0:32, ci * 128:(ci + 1) * 128], in_=PENB[:, ci * 2 + 1, :], identity=IDENT),
                 reads=[rPENB, rCB], writes=[rPTR])
        P.op(act, lambda e: e.activation(out=QM[64:96, 0, :], in_=PTR[64:96, :], func=AF.Copy), reads=[rPTR], writes=[rQM])
        P.op(act, lambda e: e.activation(out=QM[0:32, 1, :], in_=PTR[0:32, :], func=AF.Copy), reads=[rPTR], writes=[rQM])

    step = [0]
    deferred = []
    hooks = []

    def unit(t, un):
        cols = slice(t * TT, (t + 1) * TT)
        NK = 4 * t + 4
        QD, rQD = QDs[t % 2], rQDs[t % 2]
        QM, rQM = QMs[t % 2], rQMs[t % 2]
        GD, rGD = GDs[t % 2], rGDs[t % 2]
        GM, rGM = GMs[t % 2], rGMs[t % 2]
        if un == "d0":
            kf, qap, rq, rkl = (lambda kt: KD[:, kt * 128:(kt + 1) * 128]), QD[:, 0, :], rQD, rKDt
            vf = lambda kt: VD[:, kt, :]
        elif un == "d1":
            kf, qap, rq, rkl = (lambda kt: KD[:, kt * 128:(kt + 1) * 128]), QD[:, 1, :], rQD, rKDt
            vf = lambda kt: VD[:, kt, :]
        elif un == "mA":
            kf, qap, rq, rkl = (lambda kt: KM2[0:96, 0, kt * 128:(kt + 1) * 128]), QM[0:96, 0, :], rQM, rKMt
            vf = lambda kt: VA[:, kt, :]
        else:
            kf, qap, rq, rkl = (lambda kt: KM2[0:128, 1, kt * 128:(kt + 1) * 128]), QM[0:128, 1, :], rQM, rKMt
            vf = lambda kt: VB[:, kt, :]

        def lo(kt):
            return 128 * (kt - 4 * t) if kt >= 4 * t else 0

        def qk(kt):
            si = (step[0] + kt) % NSB
            diag = kt >= 4 * t
            c_lo = lo(kt)
            P.op(pe, lambda e: e.matmul(SP_[si][:, c_lo:TT], lhsT=kf(kt), rhs=qap[:, c_lo:TT], start=True, stop=(not diag)),
                 reads=[rkl[kt // 4], rq], writes=[rSP[si]], signal=(not diag))
            if diag:
                i = kt - 4 * t
                P.op(pe, lambda e: e.matmul(SP_[si][:, c_lo:TT], lhsT=IDENT, rhs=CB[:, CB_CM + i * 512 + c_lo:CB_CM + (i + 1) * 512],
                                            start=False, stop=True), reads=[rCB], writes=[rSP[si]])

        def ex(kt):
            si = (step[0] + kt) % NSB
            pi = (step[0] + kt) % NPT
            c_lo = lo(kt)
            P.op(act, lambda e: e.activation(out=PTs[pi][:, c_lo:TT], in_=SP_[si][:, c_lo:TT], func=AF.Exp, scale=0.125),
                 reads=[rSP[si]], writes=[rPTs[pi]])

        isdiff = un in ("d0", "d1")

        def pv(kt):
            pi = (step[0] + kt) % NPT
            c_lo = lo(kt)
            P.op(pe, lambda e: e.matmul(OP[:, c_lo:TT], lhsT=vf(kt), rhs=PTs[pi][:, c_lo:TT], start=(kt == 0), stop=(kt == NK - 1)),
                 reads=[rVVt[kt // 4], rPTs[pi]], writes=[rOP])
            if isdiff:
                r3 = kt % 3
                if kt < 3:
                    if c_lo > 0:
                        P.op(dve, lambda e: e.memset(ACCs[r3][:, 0:c_lo], 0.0), writes=[rACCs[r3]])
                    P.op(dve, lambda e: e.tensor_copy(out=ACCs[r3][:, c_lo:TT], in_=PTs[pi][:, c_lo:TT]), reads=[rPTs[pi]], writes=[rACCs[r3]])
                else:
                    P.op(dve, lambda e: e.tensor_tensor(out=ACCs[r3][:, c_lo:TT], in0=ACCs[r3][:, c_lo:TT], in1=PTs[pi][:, c_lo:TT], op=ALU.add),
                         reads=[rACCs[r3], rPTs[pi]], writes=[rACCs[r3]])
                if kt == NK - 1:
                    P.op(dve, lambda e: e.tensor_tensor(out=ACCs[0][:], in0=ACCs[0][:], in1=ACCs[1][:], op=ALU.add),
                         reads=[rACCs[0], rACCs[1]], writes=[rACCs[0]])
                    P.op(dve, lambda e: e.tensor_tensor(out=ACCs[0][:], in0=ACCs[0][:], in1=ACCs[2][:], op=ALU.add),
                         reads=[rACCs[0], rACCs[2]], writes=[rACCs[0]])

        qk(0)
        ex(0)
        qk(1)
        ex(1)
        for kt in range(NK):
            if kt + 2 < NK:
                qk(kt + 2)
                ex(kt + 2)
            pv(kt)
            if kt == 3:
                while hooks:
                    hooks.pop(0)()
        step[0] += NK
        if un in ("d0", "d1"):
            AX_, rAX = (A0, rA0) if un == "d0" else (A1, rA1)
            P.op(dve, lambda e: e.tensor_copy(out=AX_[:], in_=OP[:]), reads=[rOP], writes=[rAX])

            def fin_diff():
                P.op(pe, lambda e: e.matmul(LP[:], lhsT=ONESF[:], rhs=ACCs[0][:], start=True, stop=True),
                     reads=[rONESF, rACCs[0]], writes=[rLP])
                P.op(act, lambda e: e.activation(out=RC[:], in_=LP[:], func=AF.Ln), reads=[rLP], writes=[rRC])
                P.op(act, lambda e: e.activation(out=RC[:], in_=RC[:], func=AF.Exp, scale=-1.0), reads=[rRC], writes=[rRC])
                P.op(dve, lambda e: e.tensor_tensor(out=AX_[:], in0=AX_[:], in1=RC[:], op=ALU.mult), reads=[rAX, rRC], writes=[rAX])
                if un == "d1":
                    P.op(dve, lambda e: e.scalar_tensor_tensor(out=OO[:], in0=A1[:], scalar=LAMV[:, 0:1], in1=A0[:], op0=ALU.mult, op1=ALU.add),
                         reads=[rA1, rA0, rLAMV], writes=[rOO])
                    P.op(act, lambda e: e.activation(out=SQ2[:], in_=OO[:], func=AF.Square), reads=[rOO], writes=[rSQ2])
            deferred.append(fin_diff)
        if un == "d1":

            def subln_tail():
                P.op(pe, lambda e: e.matmul(MS[:], lhsT=ONESH[:], rhs=SQ2[:], start=True, stop=True), reads=[rONESH, rSQ2], writes=[rMS])
                P.op(act, lambda e: e.activation(out=SD[:], in_=MS[:], func=AF.Ln, bias=EPSV[:, 1:2], scale=1.0),
                     reads=[rMS, rEPSV], writes=[rSD])
                P.op(act, lambda e: e.activation(out=RS[:], in_=SD[:], func=AF.Exp, scale=-0.5), reads=[rSD], writes=[rRS])
                P.op(dve, lambda e: e.scalar_tensor_tensor(out=OO[:], in0=OO[:], scalar=LAMV[:, 1:2], in1=RS[:], op0=ALU.mult, op1=ALU.mult),
                     reads=[rOO, rLAMV, rRS], writes=[rOO])
                P.op(dve, lambda e: e.tensor_tensor(out=OUTA[:], in0=OO[:], in1=GD[:], op=ALU.mult), reads=[rOO, rGD], writes=[rOUTA])
                P.dma(pool, out_fn(0, t), OUTA[:], rOUTA, reads=[rOUTA])
            hooks.append(subln_tail)
        if un in ("mA", "mB"):
            hs = slice(0, 64) if un == "mA" else slice(64, 128)
            ho = slice(64, 128) if un == "mA" else slice(0, 64)
            P.op(act, lambda e: e.activation(out=RC[hs, :], in_=OP[ho, :], func=AF.Ln), reads=[rOP], writes=[rRC])
            P.op(dve, lambda e: e.tensor_copy(out=A0[hs, :], in_=OP[hs, :]), reads=[rOP], writes=[rA0])
            P.op(act, lambda e: e.activation(out=RC[hs, :], in_=RC[hs, :], func=AF.Exp, scale=-1.0), reads=[rRC], writes=[rRC])
            P.op(dve, lambda e: e.tensor_tensor(out=A0[hs, :], in0=A0[hs, :], in1=RC[hs, :], op=ALU.mult), reads=[rA0, rRC], writes=[rA0])
            P.op(dve, lambda e: e.tensor_tensor(out=OUTB[hs, :], in0=A0[hs, :], in1=GM[hs, :], op=ALU.mult),
                 reads=[rA0, rGM], writes=[rOUTB])
        if un == "mB":
            P.dma(pool, out_fn(1, t), OUTB[:], rOUTB, reads=[rOUTB])
            if dr.get("after_tile") is not None:
                dr["after_tile"](t, rOUTA, rOUTB)

    frontend(0)
    projA(0)
    projB(0)
    gatingA(0)
    gatingB(0)
    if NT > 1:
        frontend(1)
    def run_deferred():
        while deferred:
            deferred.pop(0)()

    for t in range(NT):
        unit(t, "d0")
        if t + 1 < NT:
            projA(t + 1)
        run_deferred()
        unit(t, "d1")
        if t + 1 < NT:
            projB(t + 1)
        run_deferred()
        if t + 1 < NT:
            gatingA(t + 1)
        unit(t, "mA")
        if t + 1 < NT:
            gatingB(t + 1)
        if t + 2 < NT:
            frontend(t + 2)
        unit(t, "mB")
    return [rOUTA, rOUTB]


TOK2 = S // 4
NT2 = TOK2 // TT


def build_phase2(nc, P, dr):
    pe, act, dve, pool, sp = P.pe, P.act, P.dve, P.pool, P.sp
    sb, ps, R = P.sbuf, P.psum, P.res
    WG = sb("WG", [128, 8, 2048], BF16)
    rWGs = [R(f"WG{i}", dma=True) for i in range(6)]
    WA = sb("WA", [128, 4, 1024], BF16)
    WB = sb("WB", [128, 4, 1024], BF16)
    rWA = R("WA", dma=True)
    rWB = R("WB", dma=True)
    WO = sb("WO", [128, 8, 1024], BF16)
    rWOs = [R(f"WO{i}", dma=True) for i in range(2)]
    WPG = sb("WPG", [128, 8, 1024], BF16)
    rWPGs = [R(f"WPG{i}", dma=True) for i in range(2)]
    WP = sb("WP", [128, 2, 1024], BF16); rWP = R("WP", dma=True)
    CF2 = sb("CF2", [128, 16], F32); rCF2 = R("CF2", dma=True)
    Xs = [sb("X2a", [128, 8, TT], F32), sb("X2b", [128, 8, TT], F32)]
    rXs = [R("X2a", dma=True), R("X2b", dma=True)]
    X1 = sb("X1", [128, 8, TT], F32); rX1 = R("X1")
    XN = sb("XN2", [128, 8, TT], BF16); rXN = R("XN2")
    SQ = sb("SQ2_", [128, 8, TT], BF16); rSQ = R("SQ2_")
    MG = sb("MG", [128, 8, TT], BF16); rMG = R("MG")
    X1B = sb("X1B", [128, 8, TT], BF16); rX1B = R("X1B")
    OGs = [sb("OGa", [128, 8, TT], BF16), sb("OGb", [128, 8, TT], BF16)]
    rOGs = [R("OGa", dma=True), R("OGb", dma=True)]
    PTBs = [sb("PTBa", [128, 2, TT], BF16), sb("PTBb", [128, 2, TT], BF16)]
    rPTBs = [R("PTBa", dma=True), R("PTBb", dma=True)]
    SGA = sb("SGA", [128, TT], F32); rSGA = R("SGA")
    SGB = sb("SGB", [128, TT], F32); rSGB = R("SGB")
    TA = sb("TA", [128, TT], F32); rTA = R("TA")
    TB = sb("TB", [128, TT], F32); rTB = R("TB")
    SD = sb("SD2", [128, TT], F32); rSD = R("SD2")
    RS = sb("RS2", [128, TT], F32); rRS = R("RS2")
    RSF = sb("RSF", [128, TT], F32); rRSF = R("RSF")
    SQF = [sb("SQF0", [128, TT], BF16), sb("SQF1", [128, TT], BF16)]
    rSQF = [R("SQF0"), R("SQF1")]
    OUT = [sb("OUT0", [128, TT], F32), sb("OUT1", [128, TT], F32)]
    rOUT = [R("OUT0", dma=True), R("OUT1", dma=True)]
    ONESN = sb("ONESN2", [128, 128], BF16); rONESN = R("ONESN2")
    ONESF = sb("ONESF", [128, 128], F32); rONESF = R("ONESF")
    EPSV = sb("EPSV2", [128, 1], F32); rEPSV = R("EPSV2")
    BK = [ps(f"BK{i}", [128, TT], F32) for i in range(8)]
    rBK = [R(f"BK{i}") for i in range(8)]
    bki = [0]

    def bank():
        i = bki[0] % 8
        bki[0] += 1
        return BK[i], rBK[i]

    xT2, pT, outT = dr["xT2"], dr["pT"], dr["outT"]
    P.dma(sp, CF2[:], dr["cf2"], rCF2, writes=[rCF2])
    P.dma(sp, Xs[0][:], xT2[:, :, 0:TT], rXs[0], writes=[rXs[0]])
    P.op(pool, lambda e: e.memset(ONESN[:], 1.0 / 1024.0), writes=[rONESN])
    P.op(pool, lambda e: e.memset(ONESF[:], 1.0 / 1024.0), writes=[rONESF])
    P.op(pool, lambda e: e.memset(EPSV[:], EPS), writes=[rEPSV])
    wg_ranges = [(0, 128), (1024, 1152), (128, 512), (1152, 1536), (512, 1024), (1536, 2048)]
    for i, (c0_, c1_) in enumerate(wg_ranges):
        P.dma(pool, WG[:, :, c0_:c1_], dr["wg"][:, :, c0_:c1_], rWGs[i], writes=[rWGs[i]])
        if i == 1:
            dr["og_loader"](0, OGs[0], rOGs[0])
            P.dma(pool, WA[:], dr["wa"], rWA, writes=[rWA])
            P.dma(pool, WB[:], dr["wb"], rWB, writes=[rWB])

    def wg_res(col):
        for i, (c0_, c1_) in enumerate(wg_ranges):
            if c0_ <= col < c1_:
                return rWGs[i]
        raise AssertionError(col)
    P.dma(pool, PTBs[0][:], pT[:, :, 0:TT], rPTBs[0], writes=[rPTBs[0]])
    for c in range(2):
        P.dma(pool, WO[:, :, c * 512:(c + 1) * 512], dr["wo"][:, :, c * 512:(c + 1) * 512], rWOs[c], writes=[rWOs[c]])
    for c in range(2):
        P.dma(pool, WPG[:, :, c * 512:(c + 1) * 512], dr["wpg"][:, :, c * 512:(c + 1) * 512], rWPGs[c], writes=[rWPGs[c]])
    P.dma(pool, WP[:], dr["wp"], rWP, writes=[rWP])

    def frontend(t):
        X, rX = Xs[t % 2], rXs[t % 2]
        P.op(act, lambda e: e.activation(out=SQ[:], in_=X[:], func=AF.Square), reads=[rX], writes=[rSQ])
        ms, rms = bank()
        for kc in range(8):
            P.op(pe, lambda e, kc=kc: e.matmul(ms[:], lhsT=ONESN[:], rhs=SQ[:, kc, :], start=(kc == 0), stop=(kc == 7)),
                 reads=[rONESN, rSQ], writes=[rms], signal=(kc == 7))
        P.op(act, lambda e: e.activation(out=SD[:], in_=ms[:], func=AF.Ln, bias=EPSV[:, 0:1], scale=1.0),
             reads=[rms, rEPSV], writes=[rSD])
        P.op(act, lambda e: e.activation(out=RS[:], in_=SD[:], func=AF.Exp, scale=-0.5), reads=[rSD], writes=[rRS])
        for kc in range(8):
            P.op(dve, lambda e, kc=kc: e.scalar_tensor_tensor(out=XN[:, kc, :], in0=X[:, kc, :], scalar=CF2[:, kc:kc + 1],
                                                            in1=RS[:], op0=ALU.mult, op1=ALU.mult),
                 reads=[rX, rCF2, rRS], writes=[rXN])

    frontend(0)
    oi = [0]
    for t in range(NT2):
        cols = slice(t * TT, (t + 1) * TT)
        X, rX = Xs[t % 2], rXs[t % 2]
        OG, rOG = OGs[t % 2], rOGs[t % 2]
        PTB, rPTB = PTBs[t % 2], rPTBs[t % 2]
        if t + 1 < NT2:
            n = (t + 1) % 2
            P.dma(sp, Xs[n][:], xT2[:, :, (t + 1) * TT:(t + 2) * TT], rXs[n], writes=[rXs[n]])
            dr["og_loader"](t + 1, OGs[n], rOGs[n])
            P.dma(pool, PTBs[n][:], pT[:, :, (t + 1) * TT:(t + 2) * TT], rPTBs[n], writes=[rPTBs[n]])
        for oc in range(8):
            accs = []
            for off in (0, 1024):
                pp, rpp = bank()
                for kc in range(8):
                    P.op(pe, lambda e, kc=kc, pp=pp, off=off: e.matmul(pp[:], lhsT=WG[:, kc, off + oc * 128:off + (oc + 1) * 128], rhs=XN[:, kc, :],
                                                                      start=(kc == 0), stop=(kc == 7)),
                         reads=[wg_res(off + oc * 128), rXN], writes=[rpp], signal=(kc == 7))
                accs.append((pp, rpp))
            for (ww, rww, jo) in ((WA, rWA, 0), (WB, rWB, 4)):
                pp, rpp = bank()
                for j in range(4):
                    P.op(pe, lambda e, j=j, pp=pp, ww=ww, jo=jo: e.matmul(pp[:], lhsT=ww[:, j, oc * 128:(oc + 1) * 128], rhs=OG[:, jo + j, :],
                                                                        start=(j == 0), stop=(j == 3)),
                         reads=[rww, rOG], writes=[rpp], signal=(j == 3))
                accs.append((pp, rpp))
            (pa, rpa), (pb, rpb), (pya, rpya), (pyb, rpyb) = accs
            P.op(act, lambda e: e.activation(out=SGA[:], in_=pa[:], func=AF.Sigmoid), reads=[rpa], writes=[rSGA])
            P.op(act, lambda e: e.activation(out=SGB[:], in_=pb[:], func=AF.Sigmoid), reads=[rpb], writes=[rSGB])
            P.op(dve, lambda e: e.tensor_tensor(out=TA[:], in0=pya[:], in1=SGA[:], op=ALU.mult), reads=[rpya, rSGA], writes=[rTA])
            P.op(dve, lambda e: e.tensor_tensor(out=TB[:], in0=pyb[:], in1=SGB[:], op=ALU.mult), reads=[rpyb, rSGB], writes=[rTB])
            P.op(dve, lambda e, oc=oc: e.tensor_tensor(out=MG[:, oc, :], in0=TA[:], in1=TB[:], op=ALU.add), reads=[rTA, rTB], writes=[rMG])
        for oc in range(8):
            po, rpo = bank()
            for kc in range(8):
                P.op(pe, lambda e, kc=kc: e.matmul(po[:], lhsT=WO[:, kc, oc * 128:(oc + 1) * 128], rhs=MG[:, kc, :],
                                                  start=(kc == 0), stop=(kc == 7)),
                     reads=[rWOs[oc // 4], rMG], writes=[rpo], signal=(kc == 7))
            P.op(dve, lambda e, oc=oc: e.tensor_tensor(out=X1[:, oc, :], in0=po[:], in1=X[:, oc, :], op=ALU.add),
                 reads=[rpo, rX], writes=[rX1])
            P.op(act, lambda e, oc=oc: e.activation(out=X1B[:, oc, :], in_=X1[:, oc, :], func=AF.Copy), reads=[rX1], writes=[rX1B])
        for oc in range(8):
            po, rpo = bank()
            for kc in range(8):
                P.op(pe, lambda e, kc=kc: e.matmul(po[:], lhsT=WPG[:, kc, oc * 128:(oc + 1) * 128], rhs=X1B[:, kc, :],
                                                  start=(kc == 0), stop=(kc == 7)),
                     reads=[rWPGs[oc // 4], rX1B], writes=[rpo], signal=(kc == 7))
            pp, rpp = bank()
            for kc in range(2):
                P.op(pe, lambda e, kc=kc: e.matmul(pp[:], lhsT=WP[:, kc, oc * 128:(oc + 1) * 128], rhs=PTB[:, kc, :],
                                                  start=(kc == 0), stop=(kc == 1)),
                     reads=[rWP, rPTB], writes=[rpp], signal=(kc == 1))
            P.op(act, lambda e: e.activation(out=SGA[:], in_=po[:], func=AF.Sigmoid), reads=[rpo], writes=[rSGA])
            P.op(dve, lambda e: e.tensor_tensor(out=TA[:], in0=pp[:], in1=SGA[:], op=ALU.mult), reads=[rpp, rSGA], writes=[rTA])
            P.op(dve, lambda e, oc=oc: e.tensor_tensor(out=X1[:, oc, :], in0=X1[:, oc, :], in1=TA[:], op=ALU.add),
                 reads=[rX1, rTA], writes=[rX1])
        if t + 1 < NT2:
            frontend(t + 1)
        ms, rms = bank()
        for kc in range(8):
            i = kc % 2
            P.op(act, lambda e, kc=kc, i=i: e.activation(out=SQF[i][:], in_=X1[:, kc, :], func=AF.Square), reads=[rX1], writes=[rSQF[i]])
            P.op(pe, lambda e, kc=kc, i=i: e.matmul(ms[:], lhsT=ONESN[:], rhs=SQF[i][:], start=(kc == 0), stop=(kc == 7)),
                 reads=[rONESN, rSQF[i]], writes=[rms])
        P.op(act, lambda e: e.activation(out=SD[:], in_=ms[:], func=AF.Ln, bias=EPSV[:, 0:1], scale=1.0),
             reads=[rms, rEPSV], writes=[rSD])
        P.op(act, lambda e: e.activation(out=RSF[:], in_=SD[:], func=AF.Exp, scale=-0.5), reads=[rSD], writes=[rRSF])
        for oc in range(8):
            i = oi[0] % 2
            oi[0] += 1
            P.op(dve, lambda e, oc=oc, i=i: e.scalar_tensor_tensor(out=OUT[i][:], in0=X1[:, oc, :], scalar=CF2[:, 8 + oc:9 + oc],
                                                                 in1=RSF[:], op0=ALU.mult, op1=ALU.mult),
                 reads=[rX1, rCF2, rRSF], writes=[rOUT[i]])
            P.dma(sp, outT[:, oc, cols], OUT[i][:], rOUT[i], reads=[rOUT[i]])
    return [rOUT[0], rOUT[1]]


def _rope_tables():
    inv = 500000.0 ** (-np.arange(0, 16, 2, dtype=np.float64) / 16.0)
    ang = np.arange(S, dtype=np.float64)[:, None] * inv[None, :]
    cos = np.cos(ang).astype(np.float32)
    sin = np.sin(ang).astype(np.float32)
    cs = np.zeros((128, 2, S), np.float32)
    cs[:, 0, :] = 1.0
    pm = np.zeros((128, 128), np.float32)
    for base in (0, 64):
        for i in range(16):
            j = i % 8
            p = base + i
            cs[p, 0, :] = cos[:, j]
            cs[p, 1, :] = -sin[:, j] if i < 8 else sin[:, j]
            partner = base + (i + 8 if i < 8 else i - 8)
            pm[partner, p] = 1.0
    return cs, pm


def _const_bf16():
    cb = np.zeros((128, CB_N), np.float32)
    cb[:, CB_ID:CB_ID + 128] = np.eye(128, dtype=np.float32)
    k = np.arange(128)[:, None]
    q = np.arange(512)[None, :]
    for i in range(4):
        cb[:, CB_CM + i * 512:CB_CM + (i + 1) * 512] = np.where(128 * i + k <= q, 0.0, -BIG)
    own = np.arange(32)[:, None]
    n = np.arange(32)[None, :]
    past = (n < own).astype(np.float32).reshape(-1)
    cb[:, CB_PAST:CB_PAST + 1024] = past[None, :]
    cb[:, CB_BP:CB_BP + 1024] = ((past - 1.0) * BIG)[None, :]
    return cb.astype(NPBF)


def _kchunk(w):
    C = w.shape[1]
    return np.ascontiguousarray(w.reshape(8, 128, C).transpose(1, 0, 2))


def _phase1_inputs(inp):
    x = np.asarray(inp["x"], np.float32)
    w_in = np.asarray(inp["w_in"], np.float32)[0]
    cs, pm = _rope_tables()
    cb = _const_bf16()
    oh = np.zeros((32, S), np.float32)
    for j in range(32):
        oh[j, 256 * j:256 * (j + 1)] = 1.0
    oh = oh.astype(NPBF)
    maps = []
    xTs = []
    for b in range(B):
        xt = x[b].T
        xTs.append(np.ascontiguousarray(xt.reshape(8, 128, S).transpose(1, 0, 2)))
    for c in range(N_CORES):
        b, g = c // 4, c % 4
        colsel = np.concatenate([
            np.arange(0 + g * 128, 0 + (g + 1) * 128),
            np.arange(512 + g * 128, 512 + (g + 1) * 128),
            np.arange(2048 + g * 128, 2048 + (g + 1) * 128),
            np.arange(2560 + g * 128, 2560 + (g + 1) * 128),
            np.arange(1536 + g * 128, 1536 + (g + 1) * 128),
            np.arange(3584 + g * 128, 3584 + (g + 1) * 128),
            np.arange(1024 + g * 128, 1024 + (g + 1) * 128),
            np.arange(3072 + g * 128, 3072 + (g + 1) * 128),
        ])
        w1 = _kchunk(w_in[:, colsel])
        cf = np.zeros((128, CF_N), np.float32)
        cf[:, CF_NG:CF_NG + 8] = np.asarray(inp["norm_g"], np.float32)[0].reshape(8, 128).T
        cf[:, CF_SG] = np.asarray(inp["subln_g"], np.float32)[0]
        cf[:, CF_LQ1:CF_LQ1 + 64] = np.asarray(inp["lambda_q1"], np.float32)[0][None, :]
        cf[:, CF_LK1:CF_LK1 + 64] = np.asarray(inp["lambda_k1"], np.float32)[0][None, :]
        cf[:, CF_LQ2:CF_LQ2 + 64] = np.asarray(inp["lambda_q2"], np.float32)[0][None, :]
        cf[:, CF_LK2:CF_LK2 + 64] = np.asarray(inp["lambda_k2"], np.float32)[0][None, :]
        cf[:, CF_PM:CF_PM + 128] = pm
        maps.append({"xT": xTs[b], "w1": w1, "cf": cf, "cb": cb, "cs": cs, "oh": oh})
    return maps


def build_p1_only():
    nc = bass.Bass("TRN2", target_bir_lowering=False)
    dr = {
        "xT": nc.dram_tensor("xT", [128, 8, S], F32, kind="ExternalInput").ap(),
        "w1": nc.dram_tensor("w1", [128, 8, 1024], F32, kind="ExternalInput").ap(),
        "cf": nc.dram_tensor("cf", [128, CF_N], F32, kind="ExternalInput").ap(),
        "cb": nc.dram_tensor("cb", [128, CB_N], BF16, kind="ExternalInput").ap(),
        "cs": nc.dram_tensor("cs", [128, 2, S], F32, kind="ExternalInput").ap(),
        "oh": nc.dram_tensor("oh", [32, S], BF16, kind="ExternalInput").ap(),
    }
    oT = nc.dram_tensor("oT", [256, S], BF16, kind="ExternalOutput").ap()
    dr["out_fn"] = lambda half, t: oT[half * 128:(half + 1) * 128, t * TT:(t + 1) * TT]
    P = Prog(nc)
    try:
        fin = build_phase1(nc, P, dr)
    except _Stop:
        fin = []
    P.finish(fin)
    P.close()
    return nc


def _phase2_inputs(inp, oTs):
    x = np.asarray(inp["x"], np.float32)
    p = np.asarray(inp["p"], np.float32)[0]
    w_in = np.asarray(inp["w_in"], np.float32)[0]
    wg = _kchunk(w_in[:, 4096:6144])
    wa = np.ascontiguousarray(np.asarray(inp["w_branch_diff"], np.float32)[0].reshape(4, 128, 1024).transpose(1, 0, 2))
    wb = np.ascontiguousarray(np.asarray(inp["w_branch_moba"], np.float32)[0].reshape(4, 128, 1024).transpose(1, 0, 2))
    wo = _kchunk(np.asarray(inp["w_out"], np.float32)[0])
    wpg = _kchunk(np.asarray(inp["w_ple_gate"], np.float32)[0])
    wp = np.ascontiguousarray(np.asarray(inp["w_ple"], np.float32)[0].reshape(2, 128, 1024).transpose(1, 0, 2))
    cf2 = np.zeros((128, 16), np.float32)
    cf2[:, 0:8] = np.asarray(inp["norm_g"], np.float32)[0].reshape(8, 128).T
    cf2[:, 8:16] = np.asarray(inp["final_g"], np.float32).reshape(8, 128).T
    maps = []
    for c in range(N_CORES):
        b, r = c // 4, c % 4
        ts = slice(r * TOK2, (r + 1) * TOK2)
        xT2 = np.ascontiguousarray(x[b, ts].T.reshape(8, 128, TOK2).transpose(1, 0, 2))
        pT = np.ascontiguousarray(p[b, ts].T.reshape(2, 128, TOK2).transpose(1, 0, 2))
        m = {"xT2": xT2, "pT": pT, "wg": wg, "wa": wa, "wb": wb, "wo": wo, "wpg": wpg, "wp": wp, "cf2": cf2}
        if oTs is not None:
            og = np.zeros((128, 8, TOK2), NPBF)
            for j in range(4):
                src = oTs[b * 4 + j]
                og[:, j, :] = src[0:128, ts]
                og[:, 4 + j, :] = src[128:256, ts]
            m["og"] = og
        maps.append(m)
    return maps


def _p2_drams(nc):
    return {
        "xT2": nc.dram_tensor("xT2", [128, 8, TOK2], F32, kind="ExternalInput").ap(),
        "pT": nc.dram_tensor("pT", [128, 2, TOK2], F32, kind="ExternalInput").ap(),
        "wg": nc.dram_tensor("wg", [128, 8, 2048], F32, kind="ExternalInput").ap(),
        "wa": nc.dram_tensor("wa", [128, 4, 1024], F32, kind="ExternalInput").ap(),
        "wb": nc.dram_tensor("wb", [128, 4, 1024], F32, kind="ExternalInput").ap(),
        "wo": nc.dram_tensor("wo", [128, 8, 1024], F32, kind="ExternalInput").ap(),
        "wpg": nc.dram_tensor("wpg", [128, 8, 1024], F32, kind="ExternalInput").ap(),
        "wp": nc.dram_tensor("wp", [128, 2, 1024], F32, kind="ExternalInput").ap(),
        "cf2": nc.dram_tensor("cf2", [128, 16], F32, kind="ExternalInput").ap(),
        "outT": nc.dram_tensor("outT", [128, 8, TOK2], F32, kind="ExternalOutput").ap(),
    }


def build_p2_only():
    nc = bass.Bass("TRN2", target_bir_lowering=False)
    dr = _p2_drams(nc)
    og = nc.dram_tensor("og", [128, 8, TOK2], BF16, kind="ExternalInput").ap()
    P = Prog(nc)
    dr["og_loader"] = lambda t, OG, rOG: P.dma(P.sp, OG[:], og[:, :, t * TT:(t + 1) * TT], rOG, writes=[rOG])
    fin = build_phase2(nc, P, dr)
    P.finish(fin)
    P.close()
    return nc


def _assemble(results):
    out = np.zeros((B, S, D), np.float32)
    for c in range(N_CORES):
        b, r = c // 4, c % 4
        oT = np.asarray(results[c]["outT"], np.float32)
        out[b, r * TOK2:(r + 1) * TOK2, :] = oT.transpose(2, 1, 0).reshape(TOK2, D)
    return out


def kernel_two_launch(**inp):
    maps1 = _phase1_inputs(inp)
    nc1 = build_p1_only()
    r1 = run_bass_kernel_spmd(nc1, maps1, core_ids=list(range(N_CORES)))
    oTs = [np.asarray(r1.results[c]["oT"]) for c in range(N_CORES)]
    maps2 = _phase2_inputs(inp, oTs)
    nc2 = build_p2_only()
    r2 = run_bass_kernel_spmd(nc2, maps2, core_ids=list(range(N_CORES)))
    return _assemble(r2.results)


def build_fused():
    nc = bass.Bass("TRN2", target_bir_lowering=False)
    dr1 = {
        "xT": nc.dram_tensor("xT", [128, 8, S], F32, kind="ExternalInput").ap(),
        "w1": nc.dram_tensor("w1", [128, 8, 1024], F32, kind="ExternalInput").ap(),
        "cf": nc.dram_tensor("cf", [128, CF_N], F32, kind="ExternalInput").ap(),
        "cb": nc.dram_tensor("cb", [128, CB_N], BF16, kind="ExternalInput").ap(),
        "cs": nc.dram_tensor("cs", [128, 2, S], F32, kind="ExternalInput").ap(),
        "oh": nc.dram_tensor("oh", [32, S], BF16, kind="ExternalInput").ap(),
    }
    dr2 = _p2_drams(nc)
    idx_d = nc.dram_tensor("idx", [128, 32], I32, kind="ExternalInput").ap()
    bounce = nc.dram_tensor("bounce", [NT * 256, TT], BF16)
    gathered = nc.dram_tensor("gathered", [NT * 1024, TT], BF16)
    P = Prog(nc)

    def out_fn(half, t):
        return bounce.ap()[t * 256 + half * 128:t * 256 + (half + 1) * 128, :]

    cc_sem = P.new_sem("cc")
    cc_cnt = [0]
    rGathT = [P.res(f"gath{t}") for t in range(NT)]

    def after_tile(t, rOUTA, rOUTB):
        P._waits(P.pool, [], [rOUTA, rOUTB, rGathT[t]])
        ins = nc.gpsimd.collective_compute("AllGather", ALU.bypass, replica_groups=[[0, 1, 2, 3], [4, 5, 6, 7]],
                                           ins=[bounce.ap()[t * 256:(t + 1) * 256, :]],
                                           outs=[gathered.ap()[t * 1024:(t + 1) * 1024, :]])
        cc_cnt[0] += 1
        ins.then_inc(cc_sem, 1)
        P._record((cc_sem, cc_cnt[0], "cc"), [], [rOUTA, rOUTB, rGathT[t]])

    dr1["out_fn"] = out_fn
    dr1["after_tile"] = after_tile
    IDX = P.sbuf("IDX", [128, 32], I32)
    rIDX = P.res("IDX", dma=True)
    P.dma(P.sp, IDX[:], idx_d, rIDX, writes=[rIDX])
    P.push_scope()
    fin1 = build_phase1(nc, P, dr1)
    P.pop_scope()
    P.barrier(dma_res=fin1)

    def og_loader(t, OG, rOG):
        rG = rGathT[min(12 + t, NT - 1)]
        P._waits(P.pool, [rG, rIDX], [rOG])
        for j in range(4):
            for half in range(2):
                col = (j * 2 + half) * 4 + t
                slot = j if half == 0 else 4 + j
                ins = nc.gpsimd.indirect_dma_start(out=OG[:, slot, :], out_offset=None, in_=gathered.ap(),
                                                   in_offset=bass.IndirectOffsetOnAxis(ap=IDX[:, col:col + 1], axis=0))
                rOG.dcnt += 16
                ins.then_inc(rOG.dsem, 16)
        P._record((rOG.dsem, rOG.dcnt, "d_" + rOG.name), [rG, rIDX], [rOG])

    dr2["og_loader"] = og_loader
    P.push_scope()
    fin2 = build_phase2(nc, P, dr2)
    P.finish(fin2)
    P.pop_scope()
    P.close()
    return nc


def _idx_table(r):
    idx = np.zeros((128, 32), np.int32)
    p = np.arange(128)
    for j in range(4):
        for half in range(2):
            for q in range(4):
                idx[:, (j * 2 + half) * 4 + q] = (4 * r + q) * 1024 + j * 256 + half * 128 + p
    return idx


def kernel_fused(**inp):
    maps1 = _phase1_inputs(inp)
    maps2 = _phase2_inputs(inp, None)
    maps = []
    for c in range(N_CORES):
        m = dict(maps1[c])
        m.update(maps2[c])
        m["idx"] = _idx_table(c % 4)
        maps.append(m)
    nc = build_fused()
    res = run_bass_kernel_spmd(nc, maps, core_ids=list(range(N_CORES)))
    return _assemble(res.results)


def kernel(**inp):
    return kernel_fused(**inp)
```

```python
import math
from contextlib import ExitStack

import ml_dtypes
import numpy as np

import concourse.bass as bass
import concourse.mybir as mybir
from concourse.bass_utils import run_bass_kernel_spmd

F32 = mybir.dt.float32
BF16 = mybir.dt.bfloat16
I32 = mybir.dt.int32
AF = mybir.ActivationFunctionType
ALU = mybir.AluOpType
AX = mybir.AxisListType
NPBF = ml_dtypes.bfloat16

D = 1024
S = 8192
B = 2
PLE = 256
TT = 512
NT = S // TT
BIG = 30000.0
EPS = 1e-6
SUBLN_EPS = 1e-5
LAM_INIT = 0.8 - 0.6 * math.exp(-0.3 * 0)
N_CORES = 8


class Res:
    __slots__ = ("name", "w", "r", "dsem", "dcnt")

    def __init__(self, name):
        self.name = name
        self.w = None
        self.r = {}
        self.dsem = None
        self.dcnt = 0


class EngQ:
    def __init__(self, P, eng, name):
        self.eng = eng
        self.name = name
        self.sem = P.new_sem("s_" + name)
        self.cnt = 0
        self.seen = {}
        self.pending = False


class Prog:
    def __init__(self, nc):
        self.nc = nc
        self.es = ExitStack()
        self.scopes = [self.es]
        self.pe = EngQ(self, nc.tensor, "pe")
        self.act = EngQ(self, nc.scalar, "act")
        self.dve = EngQ(self, nc.vector, "dve")
        self.pool = EngQ(self, nc.gpsimd, "pool")
        self.sp = EngQ(self, nc.sync, "sp")
        self.engs = [self.pe, self.act, self.dve, self.pool, self.sp]

    def new_sem(self, name):
        return self.es.enter_context(self.nc.semaphore(name))

    def push_scope(self):
        self.scopes.append(ExitStack())

    def pop_scope(self):
        self.scopes.pop().close()

    def sbuf(self, name, shape, dt):
        return self.scopes[-1].enter_context(self.nc.sbuf_tensor(name, shape, dt))

    def psum(self, name, shape, dt):
        return self.scopes[-1].enter_context(self.nc.psum_tensor(name, shape, dt))

    def barrier(self, dma_res=()):
        for X in self.engs:
            assert not X.pending, X.name
        for E in self.engs:
            for X in self.engs:
                if X is not E and X.cnt > 0 and E.seen.get(X.name, 0) < X.cnt:
                    E.eng.wait_ge(X.sem, X.cnt)
                    E.seen[X.name] = X.cnt
            for r in dma_res:
                key = "d_" + r.name
                if r.dcnt > 0 and E.seen.get(key, 0) < r.dcnt:
                    E.eng.wait_ge(r.dsem, r.dcnt)
                    E.seen[key] = r.dcnt

    def res(self, name, dma=False):
        r = Res(name)
        if dma:
            r.dsem = self.new_sem("d_" + name)
        return r

    def _waits(self, E, reads, writes):
        need = {}

        def add(ev):
            if ev is None:
                return
            sem, val, key = ev
            if key not in need or need[key][1] < val:
                need[key] = (sem, val)

        for b in reads:
            add(b.w)
        for b in writes:
            add(b.w)
            for ev in b.r.values():
                add(ev)
        for key, (sem, val) in need.items():
            if key == E.name and val > E.cnt:
                continue
            if E.seen.get(key, 0) < val:
                E.eng.wait_ge(sem, val)
                E.seen[key] = val

    def _record(self, ev, reads, writes):
        key = ev[2]
        for b in reads:
            old = b.r.get(key)
            if old is None or old[1] < ev[1]:
                b.r[key] = ev
        for b in writes:
            b.w = ev
            b.r = {}

    def op(self, E, fn, reads=(), writes=(), signal=True):
        self._waits(E, reads, writes)
        ins = fn(E.eng)
        if signal:
            E.cnt += 1
            ins.then_inc(E.sem, 1)
            ev = (E.sem, E.cnt, E.name)
            E.pending = False
        else:
            ev = (E.sem, E.cnt + 1, E.name)
            E.pending = True
        self._record(ev, reads, writes)
        return ev

    def dma(self, E, out, in_, dres, reads=(), writes=(), **kw):
        assert not E.pending
        self._waits(E, reads, writes)
        ins = E.eng.dma_start(out=out, in_=in_, **kw)
        dres.dcnt += 16
        ins.then_inc(dres.dsem, 16)
        ev = (dres.dsem, dres.dcnt, "d_" + dres.name)
        self._record(ev, reads, writes)
        return ev

    def finish(self, final_res):
        E = self.sp
        self._waits(E, [], final_res)
        for X in self.engs:
            assert not X.pending, X.name
            if X is not E and X.cnt > 0 and E.seen.get(X.name, 0) < X.cnt:
                E.eng.wait_ge(X.sem, X.cnt)
                E.seen[X.name] = X.cnt

    def close(self):
        self.es.close()


CF_NG = 0
CF_SG = 8
CF_LQ1 = 9
CF_LK1 = 73
CF_LQ2 = 137
CF_LK2 = 201
CF_PM = 265
CF_N = 393
CB_ID = 0
CB_CM = 128
CB_PAST = 2176
CB_BP = 3200
CB_N = 4224


STOP = None
VSTEPS = 4


class _Stop(Exception):
    pass


def _stage(k):
    if STOP is not None and STOP == k:
        raise _Stop()


def build_phase1(nc, P, dr):
    pe, act, dve, pool, sp = P.pe, P.act, P.dve, P.pool, P.sp
    sb, ps, R = P.sbuf, P.psum, P.res

    KD = sb("KD", [128, S], BF16)
    KM2 = sb("KM2", [128, 2, S], BF16); rKM2 = R("KM2", dma=True)
    rKDt = [R(f"KD{i}") for i in range(S // TT)]
    rKMt = [R(f"KM{i}") for i in range(S // TT)]
    rVVt = [R(f"VV{i}") for i in range(S // TT)]
    VD = sb("VD", [128, S // 128, 128], BF16)
    VA = sb("VA", [128, S // 128, 128], BF16)
    VB = sb("VB", [128, S // 128, 128], BF16)
    ACCs = [sb(f"ACC{i}", [128, TT], F32) for i in range(3)]
    rACCs = [R(f"ACC{i}") for i in range(3)]
    ONESF = sb("ONESF1", [128, 128], F32); rONESF = R("ONESF1")
    X = sb("X", [128, 8, TT], F32); rX = R("X", dma=True)
    SQs = [sb(f"SQ{i}", [128, TT], BF16) for i in range(2)]
    rSQs = [R(f"SQ{i}") for i in range(2)]
    XN = sb("XN", [128, 8, TT], BF16); rXN = R("XN")
    W = sb("W", [128, 8, 1024], BF16)
    rWs = [R(f"W{i}", dma=True) for i in range(4)]
    CF = sb("CF", [128, CF_N], F32); rCF = R("CF", dma=True)
    CB = sb("CB", [128, CB_N], BF16); rCB = R("CB", dma=True)
    CS = sb("CS", [128, 2, TT], F32); rCS = R("CS", dma=True)
    QDs = [sb(f"QD{i}", [128, 2, TT], BF16) for i in range(2)]
    rQDs = [R(f"QD{i}") for i in range(2)]
    QMs = [sb(f"QM{i}", [128, 2, TT], BF16) for i in range(2)]
    rQMs = [R(f"QM{i}") for i in range(2)]
    KMN = sb("KMN", [128, 32], BF16); rKMN = R("KMN")
    KMF = sb("KMF", [128, 2], F32); rKMF = R("KMF")
    ONES = sb("ONES", [128, 128], BF16); rONES = R("ONES")
    ONESN = sb("ONESN", [128, 128], BF16); rONESN = R("ONESN")
    ONESH = sb("ONESH", [128, 128], BF16); rONESH = R("ONESH")
    EPSV = sb("EPSV", [128, 2], F32); rEPSV = R("EPSV")
    ONEV = sb("ONEV", [128, 1], F32); rONEV = R("ONEV")
    LAMV = sb("LAMV", [128, 8], F32); rLAMV = R("LAMV")
    LTMP = sb("LTMP", [128, 64], F32); rLTMP = R("LTMP")
    NPT = 6
    PTs = [sb(f"PT{i}", [128, TT], BF16) for i in range(NPT)]
    rPTs = [R(f"PT{i}") for i in range(NPT)]
    QSs = [sb(f"QS{i}", [128, TT], F32) for i in range(2)]
    rQSs = [R(f"QS{i}") for i in range(2)]
    T1 = sb("T1", [128, TT], F32); rT1 = R("T1")
    T2 = sb("T2", [128, TT], F32); rT2 = R("T2")
    SD = sb("SD", [128, TT], F32); rSD = R("SD")
    RS = sb("RS", [128, TT], F32); rRS = R("RS")
    GDs = [sb(f"GD{i}", [128, TT], F32) for i in range(2)]
    rGDs = [R(f"GD{i}") for i in range(2)]
    GMs = [sb(f"GM{i}", [128, TT], F32) for i in range(2)]
    rGMs = [R(f"GM{i}") for i in range(2)]
    A0 = sb("A0", [128, TT], F32); rA0 = R("A0")
    A1 = sb("A1", [128, TT], F32); rA1 = R("A1")
    RC = sb("RC", [128, TT], F32); rRC = R("RC")
    OO = sb("OO", [128, TT], F32); rOO = R("OO")
    SQ2 = sb("SQ2", [128, TT], BF16); rSQ2 = R("SQ2")
    OUTA = sb("OUTA", [128, TT], BF16); rOUTA = R("OUTA", dma=True)
    OUTB = sb("OUTB", [128, TT], BF16); rOUTB = R("OUTB", dma=True)
    M8 = sb("M8", [128, 8, 8], F32); rM8 = R("M8")
    PENF = sb("PENF", [128, 8, 32], F32); rPENF = R("PENF")
    PENB = sb("PENB", [128, 8, 32], BF16); rPENB = R("PENB")

    NSB = 3
    SP_ = [ps(f"S{i}", [128, TT], F32) for i in range(NSB)]
    rSP = [R(f"S{i}") for i in range(NSB)]
    OP = ps("OP", [128, TT], F32); rOP = R("OP")
    LP = ps("LP", [128, TT], F32); rLP = R("LP")
    PJ = [ps("PJ0", [128, TT], F32), ps("PJ1", [128, TT], F32)]
    rPJ = [R("PJ0"), R("PJ1")]
    MS = ps("MS", [128, TT], F32); rMS = R("MS")
    GT = MS[:, 0:256]; rGT = rMS
    PTR = MS[:, 256:512].bitcast(BF16); rPTR = rMS
    assert tuple(PTR.shape) == (128, TT), PTR.shape

    xT, w1, cf, cb, cs, oh = (dr[k] for k in ("xT", "w1", "cf", "cb", "cs", "oh"))
    out_fn = dr["out_fn"]

    P.dma(sp, CF[:], cf, rCF, writes=[rCF])
    P.dma(sp, CB[:], cb, rCB, writes=[rCB])
    P.dma(sp, X[:], xT[:, :, 0:TT], rX, writes=[rX])
    P.dma(sp, CS[:], cs[:, :, 0:TT], rCS, writes=[rCS])
    for c in range(4):
        P.dma(pool, W[:, :, c * 256:(c + 1) * 256], w1[:, :, c * 256:(c + 1) * 256], rWs[c], writes=[rWs[c]])
    P.op(pool, lambda e: e.memset(ONES[:], 1.0), writes=[rONES])
    P.op(pool, lambda e: e.memset(ONESF[:], 1.0), writes=[rONESF])
    P.op(pool, lambda e: e.memset(ONESN[:], 1.0 / 1024.0), writes=[rONESN])
    P.op(pool, lambda e: e.memset(ONESH[:], 1.0 / 128.0), writes=[rONESH])
    P.op(pool, lambda e: e.memset(EPSV[:, 0:1], EPS), writes=[rEPSV])
    P.op(pool, lambda e: e.memset(ONEV[:], 1.0), writes=[rONEV])
    P.op(pool, lambda e: e.memset(EPSV[:, 1:2], SUBLN_EPS), writes=[rEPSV])
    P.op(pool, lambda e: e.memset(KM2[32:64, 1, :], 0.0), writes=[rKM2] + rKMt)
    P.op(pool, lambda e: e.memset(KMN[:], 0.0), writes=[rKMN])
    for i in range(2):
        P.op(pool, lambda e, i=i: e.memset(QMs[i][:], 0.0), writes=[rQMs[i]])
        P.op(pool, lambda e, i=i: e.memset(QDs[i][:], 0.0), writes=[rQDs[i]])
    P.dma(sp, KM2[64:96, 0, :], oh, rKM2, writes=[rKM2] + rKMt)
    P.dma(sp, KM2[0:32, 1, :], oh, rKM2, writes=[rKM2] + rKMt)
    for i, (a, b_) in enumerate(((CF_LQ1, CF_LK1), (CF_LQ2, CF_LK2))):
        P.op(dve, lambda e, a=a, b_=b_: e.tensor_tensor(out=LTMP[:], in0=CF[:, a:a + 64], in1=CF[:, b_:b_ + 64], op=ALU.mult),
             reads=[rCF], writes=[rLTMP])
        P.op(dve, lambda e, i=i: e.tensor_reduce(out=LAMV[:, 2 + i:3 + i], in_=LTMP[:], axis=AX.X, op=ALU.add),
             reads=[rLTMP], writes=[rLAMV])
        P.op(act, lambda e, i=i: e.activation(out=LAMV[:, 4 + i:5 + i], in_=LAMV[:, 2 + i:3 + i], func=AF.Exp),
             reads=[rLAMV], writes=[rLAMV])
    P.op(dve, lambda e: e.tensor_tensor(out=LAMV[:, 6:7], in0=LAMV[:, 5:6], in1=LAMV[:, 4:5], op=ALU.subtract),
         reads=[rLAMV], writes=[rLAMV])
    P.op(dve, lambda e: e.tensor_scalar(out=LAMV[:, 0:1], in0=LAMV[:, 6:7], scalar1=-LAM_INIT, scalar2=None, op0=ALU.add),
         reads=[rLAMV], writes=[rLAMV])
    P.op(dve, lambda e: e.tensor_scalar(out=LAMV[:, 1:2], in0=CF[:, CF_SG:CF_SG + 1], scalar1=1.0 - LAM_INIT, scalar2=None, op0=ALU.mult),
         reads=[rCF], writes=[rLAMV])

    IDENT = CB[:, CB_ID:CB_ID + 128]
    pj_i = [0]
    _stage(0)

    def next_pj():
        i = pj_i[0]
        pj_i[0] ^= 1
        return PJ[i], rPJ[i]

    def proj_group(col):
        pj, rpj = next_pj()
        for kc in range(8):
            P.op(pe, lambda e, kc=kc: e.matmul(pj[:], lhsT=W[:, kc, col * 128:(col + 1) * 128], rhs=XN[:, kc, :],
                                              start=(kc == 0), stop=(kc == 7)),
                 reads=[rWs[col // 2], rXN], writes=[rpj], signal=(kc == 7))
        return pj, rpj

    def rope_head(col, qi):
        pj, rpj = proj_group(col)
        P.op(act, lambda e: e.activation(out=QSs[qi][:], in_=pj[:], func=AF.Copy), reads=[rpj], writes=[rQSs[qi]])

    def rope_tail(qi, outs):
        QS_, rQS_ = QSs[qi], rQSs[qi]
        P.op(pe, lambda e: e.matmul(MS[:], lhsT=CF[:, CF_PM:CF_PM + 128], rhs=QS_[:], start=True, stop=True),
             reads=[rCF, rQS_], writes=[rMS])
        P.op(dve, lambda e: e.tensor_tensor(out=T1[:], in0=QS_[:], in1=CS[:, 0, :], op=ALU.mult),
             reads=[rQS_, rCS], writes=[rT1])
        P.op(dve, lambda e: e.tensor_tensor(out=T2[:], in0=MS[:], in1=CS[:, 1, :], op=ALU.mult),
             reads=[rMS, rCS], writes=[rT2])
        for (oap, psl, ores) in outs:
            P.op(dve, lambda e, oap=oap, psl=psl: e.tensor_tensor(out=oap, in0=T1[psl, :], in1=T2[psl, :], op=ALU.add),
                 reads=[rT1, rT2], writes=[ores])

    def frontend(t):
        c0 = t * TT
        for kc in range(8):
            i = kc % 2
            P.op(act, lambda e, kc=kc, i=i: e.activation(out=SQs[i][:], in_=X[:, kc, :], func=AF.Square), reads=[rX], writes=[rSQs[i]])
            P.op(pe, lambda e, kc=kc, i=i: e.matmul(MS[:], lhsT=ONESN[:], rhs=SQs[i][:], start=(kc == 0), stop=(kc == 7)),
                 reads=[rONESN, rSQs[i]], writes=[rMS])
        P.op(act, lambda e: e.activation(out=SD[:], in_=MS[:], func=AF.Ln, bias=EPSV[:, 0:1], scale=1.0),
             reads=[rMS, rEPSV], writes=[rSD])
        P.op(act, lambda e: e.activation(out=RS[:], in_=SD[:], func=AF.Exp, scale=-0.5), reads=[rSD], writes=[rRS])
        for kc in range(8):
            P.op(dve, lambda e, kc=kc: e.scalar_tensor_tensor(out=XN[:, kc, :], in0=X[:, kc, :], scalar=CF[:, CF_NG + kc:CF_NG + kc + 1],
                                                            in1=RS[:], op0=ALU.mult, op1=ALU.mult),
                 reads=[rX, rCF, rRS], writes=[rXN])
        if t + 1 < NT:
            P.dma(sp, X[:], xT[:, :, c0 + TT:c0 + 2 * TT], rX, writes=[rX])

    def projA(t):
        c0 = t * TT
        cols = slice(c0, c0 + TT)
        QD, rQD = QDs[t % 2], rQDs[t % 2]
        QM, rQM = QMs[t % 2], rQMs[t % 2]
        outs = [
            [(QD[0:64, 0, :], slice(0, 64), rQD), (QD[64:128, 1, :], slice(64, 128), rQD)],
            [(KD[:, cols], slice(0, 128), rKDt[t])],
            [(QM[0:64, 0, :], slice(0, 64), rQM), (QM[64:128, 1, :], slice(64, 128), rQM)],
            [(KM2[0:64, 0, cols], slice(0, 64), rKMt[t]), (KM2[64:128, 1, cols], slice(64, 128), rKMt[t])],
        ]
        rope_head(0, 0)
        rope_head(1, 1)
        rope_tail(0, outs[0])
        rope_head(2, 0)
        rope_tail(1, outs[1])
        rope_head(3, 1)
        rope_tail(0, outs[2])
        rope_tail(1, outs[3])
        if t + 1 < NT:
            P.dma(sp, CS[:], cs[:, :, c0 + TT:c0 + 2 * TT], rCS, writes=[rCS])

    def projB(t):
        cols = slice(t * TT, (t + 1) * TT)
        GD, rGD = GDs[t % 2], rGDs[t % 2]
        GM, rGM = GMs[t % 2], rGMs[t % 2]
        pj, rpj = proj_group(4)
        P.op(act, lambda e: e.activation(out=GD[:], in_=pj[:], func=AF.Silu), reads=[rpj], writes=[rGD])
        pj, rpj = proj_group(5)
        P.op(act, lambda e: e.activation(out=GM[:], in_=pj[:], func=AF.Silu), reads=[rpj], writes=[rGM])
        for st in range(4):
            pj, rpj = next_pj()
            kt = t * 4 + st
            for kc in range(8):
                P.op(pe, lambda e, kc=kc, st=st: e.matmul(pj[:, 0:192], lhsT=XN[:, kc, st * 128:(st + 1) * 128], rhs=W[:, kc, 768:960],
                                                        start=(kc == 0), stop=(kc == 7)),
                     reads=[rWs[3], rXN], writes=[rpj], signal=False)
            for kc in range(8):
                P.op(pe, lambda e, kc=kc, st=st: e.matmul(pj[:, 320:384], lhsT=XN[:, kc, st * 128:(st + 1) * 128], rhs=W[:, kc, 960:1024],
                                                        start=(kc == 0), stop=(kc == 7)),
                     reads=[rWs[3], rXN], writes=[rpj], signal=False)
            P.op(pe, lambda e: e.matmul(pj[:, 192:320], lhsT=ONES[0:1, :], rhs=ONES[0:1, :], start=True, stop=True),
                 reads=[rONES], writes=[rpj])
            P.op(act, lambda e, kt=kt: e.activation(out=VD[:, kt, :], in_=pj[:, 0:128], func=AF.Copy), reads=[rpj], writes=[rVVt[t]])
            P.op(act, lambda e, kt=kt: e.activation(out=VA[:, kt, :], in_=pj[:, 128:256], func=AF.Copy), reads=[rpj], writes=[rVVt[t]])
            P.op(act, lambda e, kt=kt: e.activation(out=VB[:, kt, :], in_=pj[:, 256:384], func=AF.Copy), reads=[rpj], writes=[rVVt[t]])
        P.op(dve, lambda e: e.tensor_reduce(out=KMF[0:64, :], in_=KM2[0:64, 0, cols].rearrange("p (b k) -> p b k", k=256),
                                            axis=AX.X, op=ALU.add), reads=[rKMt[t]], writes=[rKMF])
        P.op(dve, lambda e: e.tensor_reduce(out=KMF[64:128, :], in_=KM2[64:128, 1, cols].rearrange("p (b k) -> p b k", k=256),
                                            axis=AX.X, op=ALU.add), reads=[rKMt[t]], writes=[rKMF])
        P.op(dve, lambda e: e.tensor_scalar(out=KMN[:, 2 * t:2 * t + 2], in0=KMF[:], scalar1=1.0 / 256.0, scalar2=None, op0=ALU.mult),
             reads=[rKMF], writes=[rKMN])

    def gatingA(t):
        QM, rQM = QMs[t % 2], rQMs[t % 2]
        for ci in range(4):
            own = (4 * t + ci) // 2
            for h in range(2):
                rows = slice(0, 64) if h == 0 else slice(64, 128)
                j = ci * 2 + h
                P.op(pe, lambda e, rows=rows, h=h, ci=ci, j=j: e.matmul(GT[:, j * 32:(j + 1) * 32], lhsT=QM[rows, h, ci * 128:(ci + 1) * 128],
                                                                      rhs=KMN[rows, :], start=True, stop=False),
                     reads=[rQM, rKMN], writes=[rGT], signal=False)
                P.op(pe, lambda e, own=own, j=j: e.matmul(GT[:, j * 32:(j + 1) * 32], lhsT=ONES[0:1, :],
                                                        rhs=CB[0:1, CB_BP + own * 32:CB_BP + (own + 1) * 32], start=False, stop=True),
                     reads=[rONES, rCB], writes=[rGT])
        for j in range(8):
            P.op(dve, lambda e, j=j: e.max(out=M8[:, j, :], in_=GT[:, j * 32:(j + 1) * 32]), reads=[rGT], writes=[rM8])
        for j in range(8):
            P.op(dve, lambda e, j=j: e.tensor_scalar(out=PENF[:, j, :], in0=GT[:, j * 32:(j + 1) * 32], scalar1=M8[:, j, 2:3], scalar2=-BIG,
                                                    op0=ALU.is_lt, op1=ALU.mult), reads=[rGT, rM8], writes=[rPENF])
        for j in range(8):
            own = (4 * t + j // 2) // 2
            P.op(dve, lambda e, j=j, own=own: e.tensor_tensor(out=PENB[:, j, :], in0=PENF[:, j, :],
                                                            in1=CB[:, CB_PAST + own * 32:CB_PAST + (own + 1) * 32], op=ALU.mult),
                 reads=[rPENF, rCB], writes=[rPENB])

    def gatingB(t):
        QM, rQM = QMs[t % 2], rQMs[t % 2]
        for ci in range(4):
            P.op(pe, lambda e, ci=ci: e.transpose(PTR[64:96, ci * 128:(ci + 1) * 128], in_=PENB[:, ci * 2, :], identity=IDENT),
                 reads=[rPENB, rCB], writes=[rPTR])
            P.op(pe, lambda e, ci=ci: e.transpose(PTR[0:32, ci * 128:(ci + 1) * 128], in_=PENB[:, ci * 2 + 1, :], identity=IDENT),
                 reads=[rPENB, rCB], writes=[rPTR])
        P.op(act, lambda e: e.activation(out=QM[64:96, 0, :], in_=PTR[64:96, :], func=AF.Copy), reads=[rPTR], writes=[rQM])
        P.op(act, lambda e: e.activation(out=QM[0:32, 1, :], in_=PTR[0:32, :], func=AF.Copy), reads=[rPTR], writes=[rQM])

    step = [0]
    deferred = []
    hooks = []

    def unit(t, un):
        cols = slice(t * TT, (t + 1) * TT)
        NK = 4 * t + 4
        QD, rQD = QDs[t % 2], rQDs[t % 2]
        QM, rQM = QMs[t % 2], rQMs[t % 2]
        GD, rGD = GDs[t % 2], rGDs[t % 2]
        GM, rGM = GMs[t % 2], rGMs[t % 2]
        if un == "d0":
            kf, qap, rq, rkl = (lambda kt: KD[:, kt * 128:(kt + 1) * 128]), QD[:, 0, :], rQD, rKDt
            vf = lambda kt: VD[:, kt, :]
        elif un == "d1":
            kf, qap, rq, rkl = (lambda kt: KD[:, kt * 128:(kt + 1) * 128]), QD[:, 1, :], rQD, rKDt
            vf = lambda kt: VD[:, kt, :]
        elif un == "mA":
            kf, qap, rq, rkl = (lambda kt: KM2[0:96, 0, kt * 128:(kt + 1) * 128]), QM[0:96, 0, :], rQM, rKMt
            vf = lambda kt: VA[:, kt, :]
        else:
            kf, qap, rq, rkl = (lambda kt: KM2[0:128, 1, kt * 128:(kt + 1) * 128]), QM[0:128, 1, :], rQM, rKMt
            vf = lambda kt: VB[:, kt, :]

        def lo(kt):
            return 128 * (kt - 4 * t) if kt >= 4 * t else 0

        def qk(kt):
            si = (step[0] + kt) % NSB
            diag = kt >= 4 * t
            c_lo = lo(kt)
            P.op(pe, lambda e: e.matmul(SP_[si][:, c_lo:TT], lhsT=kf(kt), rhs=qap[:, c_lo:TT], start=True, stop=(not diag)),
                 reads=[rkl[kt // 4], rq], writes=[rSP[si]], signal=(not diag))
            if diag:
                i = kt - 4 * t
                P.op(pe, lambda e: e.matmul(SP_[si][:, c_lo:TT], lhsT=IDENT, rhs=CB[:, CB_CM + i * 512 + c_lo:CB_CM + (i + 1) * 512],
                                            start=False, stop=True), reads=[rCB], writes=[rSP[si]])

        def ex(kt):
            si = (step[0] + kt) % NSB
            pi = (step[0] + kt) % NPT
            c_lo = lo(kt)
            P.op(act, lambda e: e.activation(out=PTs[pi][:, c_lo:TT], in_=SP_[si][:, c_lo:TT], func=AF.Exp, scale=0.125),
                 reads=[rSP[si]], writes=[rPTs[pi]])

        isdiff = un in ("d0", "d1")

        def pv(kt):
            pi = (step[0] + kt) % NPT
            c_lo = lo(kt)
            P.op(pe, lambda e: e.matmul(OP[:, c_lo:TT], lhsT=vf(kt), rhs=PTs[pi][:, c_lo:TT], start=(kt == 0), stop=(kt == NK - 1)),
                 reads=[rVVt[kt // 4], rPTs[pi]], writes=[rOP])
            if isdiff:
                r3 = kt % 3
                if kt < 3:
                    if c_lo > 0:
                        P.op(dve, lambda e: e.memset(ACCs[r3][:, 0:c_lo], 0.0), writes=[rACCs[r3]])
                    P.op(dve, lambda e: e.tensor_copy(out=ACCs[r3][:, c_lo:TT], in_=PTs[pi][:, c_lo:TT]), reads=[rPTs[pi]], writes=[rACCs[r3]])
                else:
                    P.op(dve, lambda e: e.tensor_tensor(out=ACCs[r3][:, c_lo:TT], in0=ACCs[r3][:, c_lo:TT], in1=PTs[pi][:, c_lo:TT], op=ALU.add),
                         reads=[rACCs[r3], rPTs[pi]], writes=[rACCs[r3]])
                if kt == NK - 1:
                    P.op(dve, lambda e: e.tensor_tensor(out=ACCs[0][:], in0=ACCs[0][:], in1=ACCs[1][:], op=ALU.add),
                         reads=[rACCs[0], rACCs[1]], writes=[rACCs[0]])
                    P.op(dve, lambda e: e.tensor_tensor(out=ACCs[0][:], in0=ACCs[0][:], in1=ACCs[2][:], op=ALU.add),
                         reads=[rACCs[0], rACCs[2]], writes=[rACCs[0]])

        qk(0)
        ex(0)
        qk(1)
        ex(1)
        for kt in range(NK):
            if kt + 2 < NK:
                qk(kt + 2)
                ex(kt + 2)
            pv(kt)
            if kt == 3:
                while hooks:
                    hooks.pop(0)()
        step[0] += NK
        if un in ("d0", "d1"):
            AX_, rAX = (A0, rA0) if un == "d0" else (A1, rA1)
            P.op(dve, lambda e: e.tensor_copy(out=AX_[:], in_=OP[:]), reads=[rOP], writes=[rAX])

            def fin_diff():
                P.op(pe, lambda e: e.matmul(LP[:], lhsT=ONESF[:], rhs=ACCs[0][:], start=True, stop=True),
                     reads=[rONESF, rACCs[0]], writes=[rLP])
                P.op(act, lambda e: e.activation(out=RC[:], in_=LP[:], func=AF.Ln), reads=[rLP], writes=[rRC])
                P.op(act, lambda e: e.activation(out=RC[:], in_=RC[:], func=AF.Exp, scale=-1.0), reads=[rRC], writes=[rRC])
                P.op(dve, lambda e: e.tensor_tensor(out=AX_[:], in0=AX_[:], in1=RC[:], op=ALU.mult), reads=[rAX, rRC], writes=[rAX])
                if un == "d1":
                    P.op(dve, lambda e: e.scalar_tensor_tensor(out=OO[:], in0=A1[:], scalar=LAMV[:, 0:1], in1=A0[:], op0=ALU.mult, op1=ALU.add),
                         reads=[rA1, rA0, rLAMV], writes=[rOO])
                    P.op(act, lambda e: e.activation(out=SQ2[:], in_=OO[:], func=AF.Square), reads=[rOO], writes=[rSQ2])
            deferred.append(fin_diff)
        if un == "d1":

            def subln_tail():
                P.op(pe, lambda e: e.matmul(MS[:], lhsT=ONESH[:], rhs=SQ2[:], start=True, stop=True), reads=[rONESH, rSQ2], writes=[rMS])
                P.op(act, lambda e: e.activation(out=SD[:], in_=MS[:], func=AF.Ln, bias=EPSV[:, 1:2], scale=1.0),
                     reads=[rMS, rEPSV], writes=[rSD])
                P.op(act, lambda e: e.activation(out=RS[:], in_=SD[:], func=AF.Exp, scale=-0.5), reads=[rSD], writes=[rRS])
                P.op(dve, lambda e: e.scalar_tensor_tensor(out=OO[:], in0=OO[:], scalar=LAMV[:, 1:2], in1=RS[:], op0=ALU.mult, op1=ALU.mult),
                     reads=[rOO, rLAMV, rRS], writes=[rOO])
                P.op(dve, lambda e: e.tensor_tensor(out=OUTA[:], in0=OO[:], in1=GD[:], op=ALU.mult), reads=[rOO, rGD], writes=[rOUTA])
                P.dma(pool, out_fn(0, t), OUTA[:], rOUTA, reads=[rOUTA])
            hooks.append(subln_tail)
        if un in ("mA", "mB"):
            hs = slice(0, 64) if un == "mA" else slice(64, 128)
            ho = slice(64, 128) if un == "mA" else slice(0, 64)
            P.op(act, lambda e: e.activation(out=RC[hs, :], in_=OP[ho, :], func=AF.Ln), reads=[rOP], writes=[rRC])
            P.op(dve, lambda e: e.tensor_copy(out=A0[hs, :], in_=OP[hs, :]), reads=[rOP], writes=[rA0])
            P.op(act, lambda e: e.activation(out=RC[hs, :], in_=RC[hs, :], func=AF.Exp, scale=-1.0), reads=[rRC], writes=[rRC])
            P.op(dve, lambda e: e.tensor_tensor(out=A0[hs, :], in0=A0[hs, :], in1=RC[hs, :], op=ALU.mult), reads=[rA0, rRC], writes=[rA0])
            P.op(dve, lambda e: e.tensor_tensor(out=OUTB[hs, :], in0=A0[hs, :], in1=GM[hs, :], op=ALU.mult),
                 reads=[rA0, rGM], writes=[rOUTB])
        if un == "mB":
            P.dma(pool, out_fn(1, t), OUTB[:], rOUTB, reads=[rOUTB])
            if dr.get("after_tile") is not None:
                dr["after_tile"](t, rOUTA, rOUTB)

    frontend(0)
    projA(0)
    projB(0)
    gatingA(0)
    gatingB(0)
    if NT > 1:
        frontend(1)
    def run_deferred():
        while deferred:
            deferred.pop(0)()

    for t in range(NT):
        unit(t, "d0")
        if t + 1 < NT:
            projA(t + 1)
        run_deferred()
        unit(t, "d1")
        if t + 1 < NT:
            projB(t + 1)
        run_deferred()
        if t + 1 < NT:
            gatingA(t + 1)
        unit(t, "mA")
        if t + 1 < NT:
            gatingB(t + 1)
        if t + 2 < NT:
            frontend(t + 2)
        unit(t, "mB")
    return [rOUTA, rOUTB]


TOK2 = S // 4
NT2 = TOK2 // TT


def build_phase2(nc, P, dr):
    pe, act, dve, pool, sp = P.pe, P.act, P.dve, P.pool, P.sp
    sb, ps, R = P.sbuf, P.psum, P.res
    WG = sb("WG", [128, 8, 2048], BF16)
    rWGs = [R(f"WG{i}", dma=True) for i in range(6)]
    WA = sb("WA", [128, 4, 1024], BF16)
    WB = sb("WB", [128, 4, 1024], BF16)
    rWA = R("WA", dma=True)
    rWB = R("WB", dma=True)
    WO = sb("WO", [128, 8, 1024], BF16)
    rWOs = [R(f"WO{i}", dma=True) for i in range(2)]
    WPG = sb("WPG", [128, 8, 1024], BF16)
    rWPGs = [R(f"WPG{i}", dma=True) for i in range(2)]
    WP = sb("WP", [128, 2, 1024], BF16); rWP = R("WP", dma=True)
    CF2 = sb("CF2", [128, 16], F32); rCF2 = R("CF2", dma=True)
    Xs = [sb("X2a", [128, 8, TT], F32), sb("X2b", [128, 8, TT], F32)]
    rXs = [R("X2a", dma=True), R("X2b", dma=True)]
    X1 = sb("X1", [128, 8, TT], F32); rX1 = R("X1")
    XN = sb("XN2", [128, 8, TT], BF16); rXN = R("XN2")
    SQ = sb("SQ2_", [128, 8, TT], BF16); rSQ = R("SQ2_")
    MG = sb("MG", [128, 8, TT], BF16); rMG = R("MG")
    X1B = sb("X1B", [128, 8, TT], BF16); rX1B = R("X1B")
    OGs = [sb("OGa", [128, 8, TT], BF16), sb("OGb", [128, 8, TT], BF16)]
    rOGs = [R("OGa", dma=True), R("OGb", dma=True)]
    PTBs = [sb("PTBa", [128, 2, TT], BF16), sb("PTBb", [128, 2, TT], BF16)]
    rPTBs = [R("PTBa", dma=True), R("PTBb", dma=True)]
    SGA = sb("SGA", [128, TT], F32); rSGA = R("SGA")
    SGB = sb("SGB", [128, TT], F32); rSGB = R("SGB")
    TA = sb("TA", [128, TT], F32); rTA = R("TA")
    TB = sb("TB", [128, TT], F32); rTB = R("TB")
    SD = sb("SD2", [128, TT], F32); rSD = R("SD2")
    RS = sb("RS2", [128, TT], F32); rRS = R("RS2")
    RSF = sb("RSF", [128, TT], F32); rRSF = R("RSF")
    SQF = [sb("SQF0", [128, TT], BF16), sb("SQF1", [128, TT], BF16)]
    rSQF = [R("SQF0"), R("SQF1")]
    OUT = [sb("OUT0", [128, TT], F32), sb("OUT1", [128, TT], F32)]
    rOUT = [R("OUT0", dma=True), R("OUT1", dma=True)]
    ONESN = sb("ONESN2", [128, 128], BF16); rONESN = R("ONESN2")
    ONESF = sb("ONESF", [128, 128], F32); rONESF = R("ONESF")
    EPSV = sb("EPSV2", [128, 1], F32); rEPSV = R("EPSV2")
    BK = [ps(f"BK{i}", [128, TT], F32) for i in range(8)]
    rBK = [R(f"BK{i}") for i in range(8)]
    bki = [0]

    def bank():
        i = bki[0] % 8
        bki[0] += 1
        return BK[i], rBK[i]

    xT2, pT, outT = dr["xT2"], dr["pT"], dr["outT"]
    P.dma(sp, CF2[:], dr["cf2"], rCF2, writes=[rCF2])
    P.dma(sp, Xs[0][:], xT2[:, :, 0:TT], rXs[0], writes=[rXs[0]])
    P.op(pool, lambda e: e.memset(ONESN[:], 1.0 / 1024.0), writes=[rONESN])
    P.op(pool, lambda e: e.memset(ONESF[:], 1.0 / 1024.0), writes=[rONESF])
    P.op(pool, lambda e: e.memset(EPSV[:], EPS), writes=[rEPSV])
    wg_ranges = [(0, 128), (1024, 1152), (128, 512), (1152, 1536), (512, 1024), (1536, 2048)]
    for i, (c0_, c1_) in enumerate(wg_ranges):
        P.dma(pool, WG[:, :, c0_:c1_], dr["wg"][:, :, c0_:c1_], rWGs[i], writes=[rWGs[i]])
        if i == 1:
            dr["og_loader"](0, OGs[0], rOGs[0])
            P.dma(pool, WA[:], dr["wa"], rWA, writes=[rWA])
            P.dma(pool, WB[:], dr["wb"], rWB, writes=[rWB])

    def wg_res(col):
        for i, (c0_, c1_) in enumerate(wg_ranges):
            if c0_ <= col < c1_:
                return rWGs[i]
        raise AssertionError(col)
    P.dma(pool, PTBs[0][:], pT[:, :, 0:TT], rPTBs[0], writes=[rPTBs[0]])
    for c in range(2):
        P.dma(pool, WO[:, :, c * 512:(c + 1) * 512], dr["wo"][:, :, c * 512:(c + 1) * 512], rWOs[c], writes=[rWOs[c]])
    for c in range(2):
        P.dma(pool, WPG[:, :, c * 512:(c + 1) * 512], dr["wpg"][:, :, c * 512:(c + 1) * 512], rWPGs[c], writes=[rWPGs[c]])
    P.dma(pool, WP[:], dr["wp"], rWP, writes=[rWP])

    def frontend(t):
        X, rX = Xs[t % 2], rXs[t % 2]
        P.op(act, lambda e: e.activation(out=SQ[:], in_=X[:], func=AF.Square), reads=[rX], writes=[rSQ])
        ms, rms = bank()
        for kc in range(8):
            P.op(pe, lambda e, kc=kc: e.matmul(ms[:], lhsT=ONESN[:], rhs=SQ[:, kc, :], start=(kc == 0), stop=(kc == 7)),
                 reads=[rONESN, rSQ], writes=[rms], signal=(kc == 7))
        P.op(act, lambda e: e.activation(out=SD[:], in_=ms[:], func=AF.Ln, bias=EPSV[:, 0:1], scale=1.0),
             reads=[rms, rEPSV], writes=[rSD])
        P.op(act, lambda e: e.activation(out=RS[:], in_=SD[:], func=AF.Exp, scale=-0.5), reads=[rSD], writes=[rRS])
        for kc in range(8):
            P.op(dve, lambda e, kc=kc: e.scalar_tensor_tensor(out=XN[:, kc, :], in0=X[:, kc, :], scalar=CF2[:, kc:kc + 1],
                                                            in1=RS[:], op0=ALU.mult, op1=ALU.mult),
                 reads=[rX, rCF2, rRS], writes=[rXN])

    frontend(0)
    oi = [0]
    for t in range(NT2):
        cols = slice(t * TT, (t + 1) * TT)
        X, rX = Xs[t % 2], rXs[t % 2]
        OG, rOG = OGs[t % 2], rOGs[t % 2]
        PTB, rPTB = PTBs[t % 2], rPTBs[t % 2]
        if t + 1 < NT2:
            n = (t + 1) % 2
            P.dma(sp, Xs[n][:], xT2[:, :, (t + 1) * TT:(t + 2) * TT], rXs[n], writes=[rXs[n]])
            dr["og_loader"](t + 1, OGs[n], rOGs[n])
            P.dma(pool, PTBs[n][:], pT[:, :, (t + 1) * TT:(t + 2) * TT], rPTBs[n], writes=[rPTBs[n]])
        for oc in range(8):
            accs = []
            for off in (0, 1024):
                pp, rpp = bank()
                for kc in range(8):
                    P.op(pe, lambda e, kc=kc, pp=pp, off=off: e.matmul(pp[:], lhsT=WG[:, kc, off + oc * 128:off + (oc + 1) * 128], rhs=XN[:, kc, :],
                                                                      start=(kc == 0), stop=(kc == 7)),
                         reads=[wg_res(off + oc * 128), rXN], writes=[rpp], signal=(kc == 7))
                accs.append((pp, rpp))
            for (ww, rww, jo) in ((WA, rWA, 0), (WB, rWB, 4)):
                pp, rpp = bank()
                for j in range(4):
                    P.op(pe, lambda e, j=j, pp=pp, ww=ww, jo=jo: e.matmul(pp[:], lhsT=ww[:, j, oc * 128:(oc + 1) * 128], rhs=OG[:, jo + j, :],
                                                                        start=(j == 0), stop=(j == 3)),
                         reads=[rww, rOG], writes=[rpp], signal=(j == 3))
                accs.append((pp, rpp))
            (pa, rpa), (pb, rpb), (pya, rpya), (pyb, rpyb) = accs
            P.op(act, lambda e: e.activation(out=SGA[:], in_=pa[:], func=AF.Sigmoid), reads=[rpa], writes=[rSGA])
            P.op(act, lambda e: e.activation(out=SGB[:], in_=pb[:], func=AF.Sigmoid), reads=[rpb], writes=[rSGB])
            P.op(dve, lambda e: e.tensor_tensor(out=TA[:], in0=pya[:], in1=SGA[:], op=ALU.mult), reads=[rpya, rSGA], writes=[rTA])
            P.op(dve, lambda e: e.tensor_tensor(out=TB[:], in0=pyb[:], in1=SGB[:], op=ALU.mult), reads=[rpyb, rSGB], writes=[rTB])
            P.op(dve, lambda e, oc=oc: e.tensor_tensor(out=MG[:, oc, :], in0=TA[:], in1=TB[:], op=ALU.add), reads=[rTA, rTB], writes=[rMG])
        for oc in range(8):
            po, rpo = bank()
            for kc in range(8):
                P.op(pe, lambda e, kc=kc: e.matmul(po[:], lhsT=WO[:, kc, oc * 128:(oc + 1) * 128], rhs=MG[:, kc, :],
                                                  start=(kc == 0), stop=(kc == 7)),
                     reads=[rWOs[oc // 4], rMG], writes=[rpo], signal=(kc == 7))
            P.op(dve, lambda e, oc=oc: e.tensor_tensor(out=X1[:, oc, :], in0=po[:], in1=X[:, oc, :], op=ALU.add),
                 reads=[rpo, rX], writes=[rX1])
            P.op(act, lambda e, oc=oc: e.activation(out=X1B[:, oc, :], in_=X1[:, oc, :], func=AF.Copy), reads=[rX1], writes=[rX1B])
        for oc in range(8):
            po, rpo = bank()
            for kc in range(8):
                P.op(pe, lambda e, kc=kc: e.matmul(po[:], lhsT=WPG[:, kc, oc * 128:(oc + 1) * 128], rhs=X1B[:, kc, :],
                                                  start=(kc == 0), stop=(kc == 7)),
                     reads=[rWPGs[oc // 4], rX1B], writes=[rpo], signal=(kc == 7))
            pp, rpp = bank()
            for kc in range(2):
                P.op(pe, lambda e, kc=kc: e.matmul(pp[:], lhsT=WP[:, kc, oc * 128:(oc + 1) * 128], rhs=PTB[:, kc, :],
                                                  start=(kc == 0), stop=(kc == 1)),
                     reads=[rWP, rPTB], writes=[rpp], signal=(kc == 1))
            P.op(act, lambda e: e.activation(out=SGA[:], in_=po[:], func=AF.Sigmoid), reads=[rpo], writes=[rSGA])
            P.op(dve, lambda e: e.tensor_tensor(out=TA[:], in0=pp[:], in1=SGA[:], op=ALU.mult), reads=[rpp, rSGA], writes=[rTA])
            P.op(dve, lambda e, oc=oc: e.tensor_tensor(out=X1[:, oc, :], in0=X1[:, oc, :], in1=TA[:], op=ALU.add),
                 reads=[rX1, rTA], writes=[rX1])
        if t + 1 < NT2:
            frontend(t + 1)
        ms, rms = bank()
        for kc in range(8):
            i = kc % 2
            P.op(act, lambda e, kc=kc, i=i: e.activation(out=SQF[i][:], in_=X1[:, kc, :], func=AF.Square), reads=[rX1], writes=[rSQF[i]])
            P.op(pe, lambda e, kc=kc, i=i: e.matmul(ms[:], lhsT=ONESN[:], rhs=SQF[i][:], start=(kc == 0), stop=(kc == 7)),
                 reads=[rONESN, rSQF[i]], writes=[rms])
        P.op(act, lambda e: e.activation(out=SD[:], in_=ms[:], func=AF.Ln, bias=EPSV[:, 0:1], scale=1.0),
             reads=[rms, rEPSV], writes=[rSD])
        P.op(act, lambda e: e.activation(out=RSF[:], in_=SD[:], func=AF.Exp, scale=-0.5), reads=[rSD], writes=[rRSF])
        for oc in range(8):
            i = oi[0] % 2
            oi[0] += 1
            P.op(dve, lambda e, oc=oc, i=i: e.scalar_tensor_tensor(out=OUT[i][:], in0=X1[:, oc, :], scalar=CF2[:, 8 + oc:9 + oc],
                                                                 in1=RSF[:], op0=ALU.mult, op1=ALU.mult),
                 reads=[rX1, rCF2, rRSF], writes=[rOUT[i]])
            P.dma(sp, outT[:, oc, cols], OUT[i][:], rOUT[i], reads=[rOUT[i]])
    return [rOUT[0], rOUT[1]]


def _rope_tables():
    inv = (np.float32(500000.0) ** (-np.arange(0, 16, 2, dtype=np.float32) / np.float32(16))).astype(np.float32)
    ang = (np.arange(S, dtype=np.float32)[:, None] * inv[None, :]).astype(np.float32)
    cos = np.cos(ang).astype(np.float32)
    sin = np.sin(ang).astype(np.float32)
    cs = np.zeros((128, 2, S), np.float32)
    cs[:, 0, :] = 1.0
    pm = np.zeros((128, 128), np.float32)
    for base in (0, 64):
        for i in range(16):
            j = i % 8
            p = base + i
            cs[p, 0, :] = cos[:, j]
            cs[p, 1, :] = -sin[:, j] if i < 8 else sin[:, j]
            partner = base + (i + 8 if i < 8 else i - 8)
            pm[partner, p] = 1.0
    return cs, pm


def _const_bf16():
    cb = np.zeros((128, CB_N), np.float32)
    cb[:, CB_ID:CB_ID + 128] = np.eye(128, dtype=np.float32)
    k = np.arange(128)[:, None]
    q = np.arange(512)[None, :]
    for i in range(4):
        cb[:, CB_CM + i * 512:CB_CM + (i + 1) * 512] = np.where(128 * i + k <= q, 0.0, -BIG)
    own = np.arange(32)[:, None]
    n = np.arange(32)[None, :]
    past = (n < own).astype(np.float32).reshape(-1)
    cb[:, CB_PAST:CB_PAST + 1024] = past[None, :]
    cb[:, CB_BP:CB_BP + 1024] = ((past - 1.0) * BIG)[None, :]
    return cb.astype(NPBF)


def _kchunk(w):
    C = w.shape[1]
    return np.ascontiguousarray(w.reshape(8, 128, C).transpose(1, 0, 2))


def _phase1_inputs(inp):
    x = np.asarray(inp["x"], np.float32)
    w_in = np.asarray(inp["w_in"], np.float32)[0]
    cs, pm = _rope_tables()
    cb = _const_bf16()
    oh = np.zeros((32, S), np.float32)
    for j in range(32):
        oh[j, 256 * j:256 * (j + 1)] = 1.0
    oh = oh.astype(NPBF)
    maps = []
    xTs = []
    for b in range(B):
        xt = x[b].T
        xTs.append(np.ascontiguousarray(xt.reshape(8, 128, S).transpose(1, 0, 2)))
    for c in range(N_CORES):
        b, g = c // 4, c % 4
        colsel = np.concatenate([
            np.arange(0 + g * 128, 0 + (g + 1) * 128),
            np.arange(512 + g * 128, 512 + (g + 1) * 128),
            np.arange(2048 + g * 128, 2048 + (g + 1) * 128),
            np.arange(2560 + g * 128, 2560 + (g + 1) * 128),
            np.arange(1536 + g * 128, 1536 + (g + 1) * 128),
            np.arange(3584 + g * 128, 3584 + (g + 1) * 128),
            np.arange(1024 + g * 128, 1024 + (g + 1) * 128),
            np.arange(3072 + g * 128, 3072 + (g + 1) * 128),
        ])
        w1 = _kchunk(w_in[:, colsel])
        cf = np.zeros((128, CF_N), np.float32)
        cf[:, CF_NG:CF_NG + 8] = np.asarray(inp["norm_g"], np.float32)[0].reshape(8, 128).T
        cf[:, CF_SG] = np.asarray(inp["subln_g"], np.float32)[0]
        cf[:, CF_LQ1:CF_LQ1 + 64] = np.asarray(inp["lambda_q1"], np.float32)[0][None, :]
        cf[:, CF_LK1:CF_LK1 + 64] = np.asarray(inp["lambda_k1"], np.float32)[0][None, :]
        cf[:, CF_LQ2:CF_LQ2 + 64] = np.asarray(inp["lambda_q2"], np.float32)[0][None, :]
        cf[:, CF_LK2:CF_LK2 + 64] = np.asarray(inp["lambda_k2"], np.float32)[0][None, :]
        cf[:, CF_PM:CF_PM + 128] = pm
        maps.append({"xT": xTs[b], "w1": w1, "cf": cf, "cb": cb, "cs": cs, "oh": oh})
    return maps


def build_p1_only():
    nc = bass.Bass("TRN2", target_bir_lowering=False)
    dr = {
        "xT": nc.dram_tensor("xT", [128, 8, S], F32, kind="ExternalInput").ap(),
        "w1": nc.dram_tensor("w1", [128, 8, 1024], F32, kind="ExternalInput").ap(),
        "cf": nc.dram_tensor("cf", [128, CF_N], F32, kind="ExternalInput").ap(),
        "cb": nc.dram_tensor("cb", [128, CB_N], BF16, kind="ExternalInput").ap(),
        "cs": nc.dram_tensor("cs", [128, 2, S], F32, kind="ExternalInput").ap(),
        "oh": nc.dram_tensor("oh", [32, S], BF16, kind="ExternalInput").ap(),
    }
    oT = nc.dram_tensor("oT", [256, S], BF16, kind="ExternalOutput").ap()
    dr["out_fn"] = lambda half, t: oT[half * 128:(half + 1) * 128, t * TT:(t + 1) * TT]
    P = Prog(nc)
    try:
        fin = build_phase1(nc, P, dr)
    except _Stop:
        fin = []
    P.finish(fin)
    P.close()
    return nc


def _phase2_inputs(inp, oTs):
    x = np.asarray(inp["x"], np.float32)
    p = np.asarray(inp["p"], np.float32)[0]
    w_in = np.asarray(inp["w_in"], np.float32)[0]
    wg = _kchunk(w_in[:, 4096:6144])
    wa = np.ascontiguousarray(np.asarray(inp["w_branch_diff"], np.float32)[0].reshape(4, 128, 1024).transpose(1, 0, 2))
    wb = np.ascontiguousarray(np.asarray(inp["w_branch_moba"], np.float32)[0].reshape(4, 128, 1024).transpose(1, 0, 2))
    wo = _kchunk(np.asarray(inp["w_out"], np.float32)[0])
    wpg = _kchunk(np.asarray(inp["w_ple_gate"], np.float32)[0])
    wp = np.ascontiguousarray(np.asarray(inp["w_ple"], np.float32)[0].reshape(2, 128, 1024).transpose(1, 0, 2))
    cf2 = np.zeros((128, 16), np.float32)
    cf2[:, 0:8] = np.asarray(inp["norm_g"], np.float32)[0].reshape(8, 128).T
    cf2[:, 8:16] = np.asarray(inp["final_g"], np.float32).reshape(8, 128).T
    maps = []
    for c in range(N_CORES):
        b, r = c // 4, c % 4
        ts = slice(r * TOK2, (r + 1) * TOK2)
        xT2 = np.ascontiguousarray(x[b, ts].T.reshape(8, 128, TOK2).transpose(1, 0, 2))
        pT = np.ascontiguousarray(p[b, ts].T.reshape(2, 128, TOK2).transpose(1, 0, 2))
        m = {"xT2": xT2, "pT": pT, "wg": wg, "wa": wa, "wb": wb, "wo": wo, "wpg": wpg, "wp": wp, "cf2": cf2}
        if oTs is not None:
            og = np.zeros((128, 8, TOK2), NPBF)
            for j in range(4):
                src = oTs[b * 4 + j]
                og[:, j, :] = src[0:128, ts]
                og[:, 4 + j, :] = src[128:256, ts]
            m["og"] = og
        maps.append(m)
    return maps


def _p2_drams(nc):
    return {
        "xT2": nc.dram_tensor("xT2", [128, 8, TOK2], F32, kind="ExternalInput").ap(),
        "pT": nc.dram_tensor("pT", [128, 2, TOK2], F32, kind="ExternalInput").ap(),
        "wg": nc.dram_tensor("wg", [128, 8, 2048], F32, kind="ExternalInput").ap(),
        "wa": nc.dram_tensor("wa", [128, 4, 1024], F32, kind="ExternalInput").ap(),
        "wb": nc.dram_tensor("wb", [128, 4, 1024], F32, kind="ExternalInput").ap(),
        "wo": nc.dram_tensor("wo", [128, 8, 1024], F32, kind="ExternalInput").ap(),
        "wpg": nc.dram_tensor("wpg", [128, 8, 1024], F32, kind="ExternalInput").ap(),
        "wp": nc.dram_tensor("wp", [128, 2, 1024], F32, kind="ExternalInput").ap(),
        "cf2": nc.dram_tensor("cf2", [128, 16], F32, kind="ExternalInput").ap(),
        "outT": nc.dram_tensor("outT", [128, 8, TOK2], F32, kind="ExternalOutput").ap(),
    }


def build_p2_only():
    nc = bass.Bass("TRN2", target_bir_lowering=False)
    dr = _p2_drams(nc)
    og = nc.dram_tensor("og", [128, 8, TOK2], BF16, kind="ExternalInput").ap()
    P = Prog(nc)
    dr["og_loader"] = lambda t, OG, rOG: P.dma(P.sp, OG[:], og[:, :, t * TT:(t + 1) * TT], rOG, writes=[rOG])
    fin = build_phase2(nc, P, dr)
    P.finish(fin)
    P.close()
    return nc


def _assemble(results):
    out = np.zeros((B, S, D), np.float32)
    for c in range(N_CORES):
        b, r = c // 4, c % 4
        oT = np.asarray(results[c]["outT"], np.float32)
        out[b, r * TOK2:(r + 1) * TOK2, :] = oT.transpose(2, 1, 0).reshape(TOK2, D)
    return out


def kernel_two_launch(**inp):
    maps1 = _phase1_inputs(inp)
    nc1 = build_p1_only()
    r1 = run_bass_kernel_spmd(nc1, maps1, core_ids=list(range(N_CORES)))
    oTs = [np.asarray(r1.results[c]["oT"]) for c in range(N_CORES)]
    maps2 = _phase2_inputs(inp, oTs)
    nc2 = build_p2_only()
    r2 = run_bass_kernel_spmd(nc2, maps2, core_ids=list(range(N_CORES)))
    return _assemble(r2.results)


def build_fused():
    nc = bass.Bass("TRN2", target_bir_lowering=False)
    dr1 = {
        "xT": nc.dram_tensor("xT", [128, 8, S], F32, kind="ExternalInput").ap(),
        "w1": nc.dram_tensor("w1", [128, 8, 1024], F32, kind="ExternalInput").ap(),
        "cf": nc.dram_tensor("cf", [128, CF_N], F32, kind="ExternalInput").ap(),
        "cb": nc.dram_tensor("cb", [128, CB_N], BF16, kind="ExternalInput").ap(),
        "cs": nc.dram_tensor("cs", [128, 2, S], F32, kind="ExternalInput").ap(),
        "oh": nc.dram_tensor("oh", [32, S], BF16, kind="ExternalInput").ap(),
    }
    dr2 = _p2_drams(nc)
    idx_d = nc.dram_tensor("idx", [128, 32], I32, kind="ExternalInput").ap()
    bounce = nc.dram_tensor("bounce", [NT * 256, TT], BF16)
    gathered = nc.dram_tensor("gathered", [NT * 1024, TT], BF16)
    P = Prog(nc)

    def out_fn(half, t):
        return bounce.ap()[t * 256 + half * 128:t * 256 + (half + 1) * 128, :]

    cc_sem = P.new_sem("cc")
    cc_cnt = [0]
    rGathT = [P.res(f"gath{t}") for t in range(NT)]

    def after_tile(t, rOUTA, rOUTB):
        P._waits(P.pool, [], [rOUTA, rOUTB, rGathT[t]])
        ins = nc.gpsimd.collective_compute("AllGather", ALU.bypass, replica_groups=[[0, 1, 2, 3], [4, 5, 6, 7]],
                                           ins=[bounce.ap()[t * 256:(t + 1) * 256, :]],
                                           outs=[gathered.ap()[t * 1024:(t + 1) * 1024, :]])
        cc_cnt[0] += 1
        ins.then_inc(cc_sem, 1)
        P._record((cc_sem, cc_cnt[0], "cc"), [], [rOUTA, rOUTB, rGathT[t]])

    dr1["out_fn"] = out_fn
    dr1["after_tile"] = after_tile
    IDX = P.sbuf("IDX", [128, 32], I32)
    rIDX = P.res("IDX", dma=True)
    P.dma(P.sp, IDX[:], idx_d, rIDX, writes=[rIDX])
    P.push_scope()
    fin1 = build_phase1(nc, P, dr1)
    P.pop_scope()
    P.barrier(dma_res=fin1)

    def og_loader(t, OG, rOG):
        rG = rGathT[min(12 + t, NT - 1)]
        P._waits(P.pool, [rG, rIDX], [rOG])
        for j in range(4):
            for half in range(2):
                col = (j * 2 + half) * 4 + t
                slot = j if half == 0 else 4 + j
                ins = nc.gpsimd.indirect_dma_start(out=OG[:, slot, :], out_offset=None, in_=gathered.ap(),
                                                   in_offset=bass.IndirectOffsetOnAxis(ap=IDX[:, col:col + 1], axis=0))
                rOG.dcnt += 16
                ins.then_inc(rOG.dsem, 16)
        P._record((rOG.dsem, rOG.dcnt, "d_" + rOG.name), [rG, rIDX], [rOG])

    dr2["og_loader"] = og_loader
    P.push_scope()
    fin2 = build_phase2(nc, P, dr2)
    P.finish(fin2)
    P.pop_scope()
    P.close()
    return nc


def _idx_table(r):
    idx = np.zeros((128, 32), np.int32)
    p = np.arange(128)
    for j in range(4):
        for half in range(2):
            for q in range(4):
                idx[:, (j * 2 + half) * 4 + q] = (4 * r + q) * 1024 + j * 256 + half * 128 + p
    return idx


def kernel_fused(**inp):
    maps1 = _phase1_inputs(inp)
    maps2 = _phase2_inputs(inp, None)
    maps = []
    for c in range(N_CORES):
        m = dict(maps1[c])
        m.update(maps2[c])
        m["idx"] = _idx_table(c % 4)
        maps.append(m)
    nc = build_fused()
    res = run_bass_kernel_spmd(nc, maps, core_ids=list(range(N_CORES)))
    return _assemble(res.results)


def kernel(**inp):
    return kernel_fused(**inp)
```
